# Optimizing a Trainium2 kernel written in Bass

```python
import jax, jax.numpy as jnp
from jax import lax
import numpy as np

D_MODEL = 1024
BATCH = 16
SEQ = 2048
DEPTH = 1

PLE_DIM = 256
MIX_WIDTH = D_MODEL
NORM_EPS = 1e-6
D_FF = 4 * D_MODEL

GLA_WIDTH = MIX_WIDTH // 2
GLA_HEADS = 4
GLA_DV = GLA_WIDTH // GLA_HEADS
GLA_DK = GLA_DV // 2
GLA_KEY_WIDTH = GLA_HEADS * GLA_DK
GLA_GATE_RANK = 16
GLA_GATE_NORMALIZER = 16.0
GLA_CONV = 4
GLA_CHUNK = 64
GLA_COLS = 2 * GLA_KEY_WIDTH + 2 * GLA_WIDTH + GLA_GATE_RANK
GLA_SPLITS = (GLA_KEY_WIDTH, 2 * GLA_KEY_WIDTH, 2 * GLA_KEY_WIDTH + GLA_WIDTH, 2 * GLA_KEY_WIDTH + 2 * GLA_WIDTH)

RWKV_WIDTH = MIX_WIDTH - GLA_WIDTH
RWKV_HEAD = 64
RWKV_HEADS = RWKV_WIDTH // RWKV_HEAD
RWKV_DECAY_RANK = 32
RWKV_AAA_RANK = 32
RWKV_GATE_RANK = 96
RWKV_GN_EPS = 64e-5
RWKV_COLS = 3 * RWKV_WIDTH + RWKV_DECAY_RANK + RWKV_AAA_RANK + RWKV_GATE_RANK
RWKV_SPLITS = (RWKV_WIDTH, 2 * RWKV_WIDTH, 3 * RWKV_WIDTH, 3 * RWKV_WIDTH + RWKV_DECAY_RANK, 3 * RWKV_WIDTH + RWKV_DECAY_RANK + RWKV_AAA_RANK)

D_IN = GLA_COLS + RWKV_COLS

kernel_name = "hymba_gla_rwkv7_hybrid_block"


def _rmsnorm(x, g):
    xf = x.astype(jnp.float32)
    y = xf * lax.rsqrt(jnp.mean(xf * xf, axis=-1, keepdims=True) + NORM_EPS)
    return (y * g.astype(jnp.float32)).astype(x.dtype)


def _causal_dwconv(x, w):
    k = w.shape[0]
    return lax.conv_general_dilated(
        x, w[:, None, :].astype(x.dtype), window_strides=(1,), padding=[(k - 1, 0)],
        dimension_numbers=('NWC', 'WIO', 'NWC'), feature_group_count=x.shape[-1])


def _token_shift(y, mu):
    y_prev = jnp.pad(y, ((0, 0), (1, 0), (0, 0)))[:, :-1]
    return y + mu * (y_prev - y)


def _gla_mixer(z, conv_w, gk_w, gk_b, norm_g):
    B, S, _ = z.shape
    f32 = jnp.float32
    q, k, v, g, gk_low = jnp.split(z, GLA_SPLITS, axis=-1)
    qkv = jax.nn.silu(_causal_dwconv(jnp.concatenate([q, k, v], axis=-1), conv_w))
    q, k, v = jnp.split(qkv, (GLA_KEY_WIDTH, 2 * GLA_KEY_WIDTH), axis=-1)
    gk = jax.nn.log_sigmoid((gk_low @ gk_w + gk_b).astype(f32)) / GLA_GATE_NORMALIZER
    n_chunks = S // GLA_CHUNK

    def chunks(t, d):
        t = t.astype(f32).reshape(B, n_chunks, GLA_CHUNK, GLA_HEADS, d)
        return jnp.transpose(t, (1, 0, 3, 2, 4))

    qc = chunks(q, GLA_DK) * (GLA_DK ** -0.5)
    kc = chunks(k, GLA_DK)
    vc = chunks(v, GLA_DV)
    gc = chunks(gk, GLA_DK)
    causal = jnp.tril(jnp.ones((GLA_CHUNK, GLA_CHUNK), dtype=bool))

    def step(state, inp):
        qb, kb, vb, gb = inp
        b = jnp.cumsum(gb, axis=2)
        diff = b[:, :, :, None, :] - b[:, :, None, :, :]
        decay = jnp.exp(jnp.where(causal[:, :, None], diff, -jnp.inf))
        scores = jnp.einsum('bhid,bhjd,bhijd->bhij', qb, kb, decay)
        o = (jnp.einsum('bhij,bhje->bhie', scores, vb)
             + jnp.einsum('bhid,bhde->bhie', qb * jnp.exp(b), state))
        b_last = b[:, :, -1:, :]
        state = (jnp.exp(b_last[:, :, 0, :])[..., None] * state
                 + jnp.einsum('bhjd,bhje->bhde', kb * jnp.exp(b_last - b), vb))
        return state, o

    state0 = jnp.zeros((B, GLA_HEADS, GLA_DK, GLA_DV), f32)
    _, o = lax.scan(step, state0, (qc, kc, vc, gc))
    o = jnp.transpose(o, (1, 0, 3, 2, 4)).reshape(B, S, GLA_HEADS, GLA_DV)
    o = o * lax.rsqrt(jnp.mean(o * o, axis=-1, keepdims=True) + NORM_EPS) * norm_g.astype(f32)
    return o.reshape(B, S, GLA_WIDTH).astype(z.dtype) * jax.nn.silu(g)


def _rwkv7_mixer(z, mu, w0, w2, a0, a2, g2, k_k, k_a, r_k, gn_w, gn_b):
    B, S, _ = z.shape
    f32 = jnp.float32
    z = _token_shift(z, mu)
    r, k, v, w_low, a_low, g_low = jnp.split(z, RWKV_SPLITS, axis=-1)
    w_log = -jax.nn.softplus(-(w0 + jnp.tanh(w_low) @ w2).astype(f32)) - 0.5
    decay = jnp.exp(-jnp.exp(w_log))
    a = jax.nn.sigmoid((a0 + a_low @ a2).astype(f32))
    g = jax.nn.sigmoid(g_low) @ g2

    def heads(t):
        return t.astype(f32).reshape(B, S, RWKV_HEADS, RWKV_HEAD)

    kk = heads(k * k_k)
    kk = kk / jnp.maximum(jnp.sqrt(jnp.sum(kk * kk, axis=-1, keepdims=True)), 1e-12)
    kf = k.astype(f32) * (1.0 + (a - 1.0) * k_a.astype(f32))
    rh, kh, vh, wh, ah = heads(r), heads(kf), heads(v), heads(decay), heads(a)

    def tm(t):
        return jnp.moveaxis(t, 1, 0)

    def step(state, inp):
        r_t, w_t, k_t, v_t, a_t, b_t = inp
        sa = jnp.einsum('bhij,bhj->bhi', state, a_t)
        state = (state * w_t[:, :, None, :] + sa[..., None] * b_t[:, :, None, :]
                 + v_t[..., None] * k_t[:, :, None, :])
        return state, jnp.einsum('bhij,bhj->bhi', state, r_t)

    state0 = jnp.zeros((B, RWKV_HEADS, RWKV_HEAD, RWKV_HEAD), f32)
    _, y = lax.scan(step, state0, (tm(rh), tm(wh), tm(kh), tm(vh), tm(-kk), tm(kk * ah)))
    y = jnp.moveaxis(y, 0, 1)
    mean = jnp.mean(y, axis=-1, keepdims=True)
    var = jnp.mean(jnp.square(y - mean), axis=-1, keepdims=True)
    y = ((y - mean) * lax.rsqrt(var + RWKV_GN_EPS)).reshape(B, S, RWKV_WIDTH)
    y = y * gn_w.astype(f32) + gn_b.astype(f32)
    bonus = jnp.sum(rh * kh * r_k.astype(f32), axis=-1, keepdims=True) * vh
    y = y + bonus.reshape(B, S, RWKV_WIDTH)
    return y.astype(z.dtype) * g


def setup_inputs(seed: int = 0) -> dict:
    key = jax.random.key(seed)
    ks = jax.random.split(key, 32)
    nrm = lambda k, shape, scale: jax.random.normal(k, shape, jnp.float32) * scale
    gain = lambda k, shape: 1.0 + 0.02 * jax.random.normal(k, shape, jnp.float32)
    L = DEPTH
    return {
        "x": nrm(ks[0], (BATCH, SEQ, D_MODEL), 1.0),
        "p": nrm(ks[1], (DEPTH, BATCH, SEQ, PLE_DIM), 1.0),
        "ln_mix": gain(ks[2], (L, D_MODEL)),
        "w_in": nrm(ks[3], (L, D_MODEL, D_IN), D_MODEL ** -0.5),
        "gla_conv_w": nrm(ks[4], (L, GLA_CONV, 2 * GLA_KEY_WIDTH + GLA_WIDTH), GLA_CONV ** -0.5),
        "gla_gk_w": nrm(ks[5], (L, GLA_GATE_RANK, GLA_KEY_WIDTH), GLA_GATE_RANK ** -0.5),
        "gla_gk_b": nrm(ks[6], (L, GLA_KEY_WIDTH), 0.1),
        "gla_norm_g": gain(ks[7], (L, GLA_DV)),
        "rwkv_mu": jax.random.uniform(ks[8], (L, RWKV_COLS), jnp.float32, 0.0, 1.0),
        "rwkv_w0": jax.random.uniform(ks[9], (L, RWKV_WIDTH), jnp.float32, -6.0, -1.0),
        "rwkv_w2": nrm(ks[10], (L, RWKV_DECAY_RANK, RWKV_WIDTH), 0.5 * RWKV_DECAY_RANK ** -0.5),
        "rwkv_a0": nrm(ks[11], (L, RWKV_WIDTH), 0.1),
        "rwkv_a2": nrm(ks[12], (L, RWKV_AAA_RANK, RWKV_WIDTH), RWKV_AAA_RANK ** -0.5),
        "rwkv_g2": nrm(ks[13], (L, RWKV_GATE_RANK, RWKV_WIDTH), RWKV_GATE_RANK ** -0.5),
        "rwkv_k_k": 0.85 + nrm(ks[14], (L, RWKV_WIDTH), 0.02),
        "rwkv_k_a": gain(ks[15], (L, RWKV_WIDTH)),
        "rwkv_r_k": nrm(ks[16], (L, RWKV_HEADS, RWKV_HEAD), 0.1),
        "rwkv_gn_w": gain(ks[17], (L, RWKV_WIDTH)),
        "rwkv_gn_b": nrm(ks[18], (L, RWKV_WIDTH), 0.02),
        "w_out": nrm(ks[19], (L, MIX_WIDTH, D_MODEL), MIX_WIDTH ** -0.5),
        "ln_mlp": gain(ks[20], (L, D_MODEL)),
        "w_ff1": nrm(ks[21], (L, D_MODEL, D_FF), D_MODEL ** -0.5),
        "w_ff2": nrm(ks[22], (L, D_FF, D_MODEL), D_FF ** -0.5),
        "ln_ple": gain(ks[23], (L, D_MODEL)),
        "w_ple_gate": nrm(ks[24], (L, D_MODEL, D_MODEL), D_MODEL ** -0.5),
        "w_ple_proj": nrm(ks[25], (L, PLE_DIM, D_MODEL), PLE_DIM ** -0.5),
        "ln_final": gain(ks[26], (D_MODEL,)),
    }


def reference(x, p, ln_mix, w_in, gla_conv_w, gla_gk_w, gla_gk_b, gla_norm_g,
              rwkv_mu, rwkv_w0, rwkv_w2, rwkv_a0, rwkv_a2, rwkv_g2, rwkv_k_k, rwkv_k_a,
              rwkv_r_k, rwkv_gn_w, rwkv_gn_b, w_out, ln_mlp, w_ff1, w_ff2,
              ln_ple, w_ple_gate, w_ple_proj, ln_final):
    h = x
    for i in range(DEPTH):
        z = _rmsnorm(h, ln_mix[i]) @ w_in[i]
        z_gla, z_rwkv = z[..., :GLA_COLS], z[..., GLA_COLS:]
        o_gla = _gla_mixer(z_gla, gla_conv_w[i], gla_gk_w[i], gla_gk_b[i], gla_norm_g[i])
        o_rwkv = _rwkv7_mixer(z_rwkv, rwkv_mu[i], rwkv_w0[i], rwkv_w2[i], rwkv_a0[i],
                              rwkv_a2[i], rwkv_g2[i], rwkv_k_k[i], rwkv_k_a[i],
                              rwkv_r_k[i], rwkv_gn_w[i], rwkv_gn_b[i])
        h = h + jnp.concatenate([o_gla, o_rwkv], axis=-1) @ w_out[i]
        h = h + jnp.square(jax.nn.relu(_rmsnorm(h, ln_mlp[i]) @ w_ff1[i])) @ w_ff2[i]
        gate = jax.nn.sigmoid(_rmsnorm(h, ln_ple[i]) @ w_ple_gate[i])
        h = h + gate * (p[i] @ w_ple_proj[i])
    return _rmsnorm(h, ln_final)
```

```python
import numpy as np
import concourse.bass as bass
import concourse.mybir as mybir
from concourse.bass_utils import run_bass_kernel_spmd
from contextlib import ExitStack

F32 = mybir.dt.float32
BF16 = mybir.dt.bfloat16
AF = mybir.ActivationFunctionType
ALU = mybir.AluOpType

NCORES = 8
TOK = 4096
NB = 512
NBLK = TOK // NB
CDEC = float(np.exp(-0.5))
INTERLEAVE_PROLOGUE = False
POOL_E = "dve"


class T:
    __slots__ = ("t", "w", "r", "name")

    def __init__(self, t, name=""):
        self.t = t
        self.w = None
        self.r = {}
        self.name = name

    def __getitem__(self, k):
        return self.t[k]


class Sched:
    NDMA = 56

    def __init__(self, nc, es):
        self.nc = nc
        self.eng = {"pe": nc.tensor, "act": nc.scalar, "dve": nc.vector,
                    "pool": nc.gpsimd, "sp": nc.sync}
        self.sem = {k: es.enter_context(nc.semaphore("s_" + k)) for k in self.eng}
        self.cnt = {k: 0 for k in self.eng}
        self.pend = {k: False for k in self.eng}
        self.seen = {k: {} for k in self.eng}
        self.dsem = [es.enter_context(nc.semaphore("d%d" % i)) for i in range(self.NDMA)]
        self.dcnt = [0] * self.NDMA
        self.dnext = 0
        self.ninst = 0
        self.frontier = {}

    def _semof(self, key):
        return self.dsem[key] if isinstance(key, int) else self.sem[key]

    def _wait(self, e, key, idx):
        if idx is None or idx <= 0:
            return
        if self.seen[e].get(key, 0) >= idx:
            return
        if not isinstance(key, int):
            assert idx <= self.cnt[key], ("wait on unsignaled", e, key, idx, self.cnt[key])
        self.eng[e].wait_ge(self._semof(key), idx)
        self.seen[e][key] = idx

    def _deps(self, e, reads, writes, is_dma=False):
        for t in reads:
            if t.w is not None:
                self._wait(e, *t.w)
        for t in writes:
            if t.w is not None:
                if not (e == "pe" and t.w[0] == "pe"):
                    self._wait(e, *t.w)
            for k, idx in t.r.items():
                if k == e == "pe":
                    continue
                self._wait(e, k, idx)

    def op(self, e, fn, reads=(), writes=(), signal=True):
        self._deps(e, reads, writes)
        ins = fn()
        self.ninst += 1
        if signal:
            ins.then_inc(self.sem[e], 1)
            self.cnt[e] += 1
            idx = self.cnt[e]
            self.pend[e] = False
        else:
            idx = self.cnt[e] + 1
            self.pend[e] = True
        for t in writes:
            t.w = (e, idx)
            t.r = {}
        for t in reads:
            if t.r.get(e, 0) < idx:
                t.r[e] = idx
        return ins

    def dma(self, q, out_ap, in_ap, reads=(), writes=()):
        s = self.dnext
        self.dnext = (self.dnext + 1) % self.NDMA
        self._wait(q, s, self.dcnt[s])
        self._deps(q, reads, writes, True)
        ins = self.eng[q].dma_start(out=out_ap, in_=in_ap)
        ins.then_inc(self.dsem[s], 16)
        self.dcnt[s] += 16
        idx = self.dcnt[s]
        for t in writes:
            t.w = (s, idx)
            t.r = {}
        for t in reads:
            t.r[s] = idx
        return s, idx

    def dma_fresh(self, es, q, out_ap, in_ap, reads=(), writes=()):
        s = len(self.dsem)
        self.dsem.append(es.enter_context(self.nc.semaphore("f%d" % s)))
        self.dcnt.append(0)
        self._deps(q, reads, writes, True)
        ins = self.eng[q].dma_start(out=out_ap, in_=in_ap)
        ins.then_inc(self.dsem[s], 16)
        self.dcnt[s] = 16
        for t in writes:
            t.w = (s, 16)
            t.r = {}
        for t in reads:
            t.r[s] = 16
        return s, 16

    def release_point(self, engs=("pe", "act", "dve", "pool")):
        for e in engs:
            assert not self.pend[e], ("release with unsignaled tail", e)
            self.frontier[e] = self.cnt[e]

    def barrier(self, engs=("pe", "act", "dve")):
        for e in engs:
            assert not self.pend[e], ("barrier with unsignaled tail", e)
        for e in engs:
            for k in engs:
                if k != e:
                    self._wait(e, k, self.cnt[k])

    def wait_all_dma(self, q="sp"):
        for s in range(len(self.dsem)):
            self._wait(q, s, self.dcnt[s])


class _Skip(Exception):
    pass


class Phase:
    def __init__(self, S, nc):
        self.S, self.nc = S, nc
        self.es = ExitStack()
        self.n = 0

    def __enter__(self):
        self.es.__enter__()
        return self

    CNT = [0]

    def sb(self, name, shape, dt):
        Phase.CNT[0] += 1
        t = T(self.es.enter_context(self.nc.sbuf_tensor("%s_%d" % (name, Phase.CNT[0]), shape, dt)), name)
        t.r = dict(self.S.frontier)
        return t

    def __exit__(self, *a):
        try:
            if a[0] is None or a[0] is _Skip:
                self.S.release_point()
        finally:
            if a[0] is _Skip:
                self.es.__exit__(None, None, None)
                r = False
            else:
                r = self.es.__exit__(*a)
        return r


def _const_arrays():
    p = np.arange(128)
    pl = p % 64
    ident = np.eye(128, dtype=np.float32)
    ones = np.ones((128, 128), np.float32)
    bd = (p[:, None] // 64 == p[None, :] // 64).astype(np.float32)
    i64 = np.arange(64)
    maskg = np.tile((pl[:, None] <= i64[None, :]).astype(np.float32), (1, 4))
    strict = (pl[:, None] < i64[None, :]).astype(np.float32)
    incl = (pl[:, None] <= i64[None, :]).astype(np.float32)
    mrw = np.tile(np.concatenate([strict, incl], axis=1), (1, 4))
    mq = np.tile((i64[None, :] < pl[:, None]).astype(np.float32), (1, 8))
    i8 = np.tile((i64[None, :] == pl[:, None]).astype(np.float32), (1, 8))
    cb = np.concatenate([ident, ones, bd, maskg, mrw, mq, i8], axis=1)
    mscan = np.ones((128, 512), np.float32)
    mscan[:, 0::64] = 0.0
    cf = np.concatenate([ident, mscan], axis=1)
    return np.ascontiguousarray(cb), np.ascontiguousarray(cf)


CB_ID, CB_ONES, CB_BD, CB_MG, CB_MRW, CB_MQ, CB_I8 = 0, 128, 256, 384, 640, 1152, 1664
CB_W = 2176

C_LNMIX, C_LNMLP, C_LNPLE, C_LNF, C_CONV, C_GKB, C_NG, C_MU = 0, 8, 16, 24, 32, 64, 66, 67
C_W0, C_A0, C_KK, C_KA, C_RK, C_GNW, C_GNB, C_MULR, C_MUG = 79, 83, 87, 91, 95, 99, 103, 107, 108
C_NGKB, C_1MKA = 109, 111


def _pack_vecs(i):
    rows = np.zeros((128, 128), np.float32)

    def put(r0, v):
        v = np.asarray(v, np.float32).reshape(-1)
        n = (v.size + 127) // 128
        buf = np.zeros(n * 128, np.float32)
        buf[:v.size] = v
        rows[r0:r0 + n] = buf.reshape(n, 128)

    put(C_LNMIX, i["ln_mix"]); put(C_LNMLP, i["ln_mlp"]); put(C_LNPLE, i["ln_ple"]); put(C_LNF, i["ln_final"])
    put(C_CONV, i["gla_conv_w"]); put(C_GKB, i["gla_gk_b"]); put(C_NG, i["gla_norm_g"])
    mu = np.asarray(i["rwkv_mu"], np.float32).reshape(-1)
    put(C_MU, mu[:1536]); put(C_MULR, mu[1536:1600]); put(C_MUG, mu[1600:1696])
    put(C_W0, i["rwkv_w0"]); put(C_A0, i["rwkv_a0"]); put(C_KK, i["rwkv_k_k"]); put(C_KA, i["rwkv_k_a"])
    put(C_RK, i["rwkv_r_k"]); put(C_GNW, i["rwkv_gn_w"]); put(C_GNB, i["rwkv_gn_b"])
    return np.ascontiguousarray(rows.T)


def build_nc():
    nc = bass.Bass("TRN2", target_bir_lowering=False)
    import os
    nblk = int(os.environ.get("KNBLK", NBLK))
    stage = int(os.environ.get("KSTAGE", 99))

    ksub = int(os.environ.get("KSUB", 99))

    def sub(n):
        if ksub == n:
            raise _Skip()

    def chk(n):
        if stage == n:
            raise _Skip()

    def din(name, shape):
        return nc.dram_tensor(name, shape, F32, kind="ExternalInput").ap()

    x = din("x", [TOK, 1024])
    pin = din("p", [TOK, 256])
    w_in = din("w_in", [1024, 3248])
    w_out = din("w_out", [1024, 1024])
    w_ff1 = din("w_ff1", [1024, 4096])
    w_ff2 = din("w_ff2", [4096, 1024])
    w_pg = din("w_pg", [1024, 1024])
    w_pp = din("w_pp", [256, 1024])
    gkw_d = din("gk_w", [16, 256])
    w2_d = din("w2", [32, 512])
    a2_d = din("a2", [32, 512])
    g2_d = din("g2", [96, 512])
    vecs_d = din("vecs", [128, 128])
    cb_d = din("cb", [128, CB_W])
    cf_d = din("cf", [128, 640])
    out = nc.dram_tensor("out", [TOK, 1024], F32, kind="ExternalOutput").ap()

    with ExitStack() as es:
        S = Sched(nc, es)

        def sb(name, shape, dt):
            return T(es.enter_context(nc.sbuf_tensor(name, shape, dt)), name)

        banks = [T(es.enter_context(nc.psum_tensor("ps%d" % i, [128, 512], F32)), "ps%d" % i) for i in range(8)]
        bstate = [0]
        bank_hist = {}

        def bank():
            b = banks[bstate[0] % 8]
            bstate[0] += 1
            bank_hist.pop(id(b), None)
            return b

        def bf(b):
            return b.t[:].bitcast(BF16)

        CC = sb("CC", [128, 128], F32)
        CBt = sb("CB", [128, CB_W], BF16)
        CF = sb("CF", [128, 640], F32)
        GKW = sb("GKW", [16, 256], BF16)
        W2T = sb("W2T", [32, 512], BF16)
        A2T = sb("A2T", [64, 512], BF16)
        G2T = sb("G2T", [96, 512], BF16)
        NWB = 4
        WBF = [sb("WB%d" % i, [128, 8, 512], BF16) for i in range(NWB)]
        WB = [[T(WBF[i][:, k, :], "WB%d_%d" % (i, k)) for k in range(8)] for i in range(NWB)]
        HT = [sb("HT0", [128, 8, NB], F32)] * 2
        Hs = [[T(HT[0][:, k, :], "H0_%d" % k) for k in range(8)]] * 2
        PTs = [sb("PT0", [128, 2, NB], BF16)] * 2
        XN1 = sb("XN1", [128, 8, NB], BF16)
        XN1K = [T(XN1[:, k, :], "XN1_k%d" % k) for k in range(8)]
        hcur = [None]
        XS = [sb("XS%d" % i, [128, 1024], F32) for i in range(2)]
        PSt = [sb("PSt%d" % i, [128, 256], F32) for i in range(2)]
        GC = sb("GC", [128, 8, 2, 4], BF16)
        RC = sb("RC", [128, 14, 2, 4], F32)
        GST = [sb("GST%d" % i, [128, 2, 128], F32) for i in range(2)]
        GSTB = [sb("GSTB%d" % i, [128, 2, 128], BF16) for i in range(2)]
        TS = [sb("TS%d" % i, [128, 4, 64], F32) for i in range(2)]
        TSB = [[sb("TSB%d_%d" % (i, q), [128, 4, 64], BF16) for q in range(2)] for i in range(2)]

        IDB = CBt[:, CB_ID:CB_ID + 128]
        ONES = CBt[:, CB_ONES:CB_ONES + 128]
        BD = CBt[:, CB_BD:CB_BD + 128]
        IDF = CF[:, 0:128]
        MSCAN = CF[:, 128:640]

        def mm(oT, o_ap, lT, l_ap, rT, r_ap, start=True, stop=True, sig=False, force=True):
            small = l_ap.partition_size() <= 64
            if small:
                sig = sig or force
                rg = l_ap.base_partition()
                lo = o_ap.base_partition()
                hi = lo + o_ap.partition_size()
                hist = bank_hist.setdefault(id(oT), [])
                for (rg2, lo2, hi2, idx2) in hist:
                    if rg2 != rg and lo < hi2 and lo2 < hi:
                        S._wait("pe", "pe", idx2)
                hist[:] = [e for e in hist if S.seen["pe"].get("pe", 0) < e[3]]
            else:
                bank_hist.pop(id(oT), None)
            S.op("pe", lambda: nc.tensor.matmul(o_ap, lhsT=l_ap, rhs=r_ap, start=start, stop=stop),
                 reads=[lT, rT], writes=[oT], signal=sig)
            if small:
                hist.append((rg, lo, hi, S.cnt["pe"] if sig else S.cnt["pe"] + 1))

        def tr(oT, o_ap, iT, i_ap, idT, id_ap, sig=False):
            S.op("pe", lambda: nc.tensor.transpose(out=o_ap, in_=i_ap, identity=id_ap),
                 reads=[iT, idT], writes=[oT], signal=sig)

        def act(func, oT, o_ap, iT, i_ap, scale=1.0, bias=None, rd=()):
            def f():
                if bias is None:
                    return nc.scalar.activation(out=o_ap, in_=i_ap, func=func, scale=scale)
                return nc.scalar.activation(out=o_ap, in_=i_ap, func=func, scale=scale, bias=bias)
            S.op("act", f, reads=[iT, *rd], writes=[oT])

        def tt(oT, o_ap, aT, a_ap, bT, b_ap, op, e="dve"):
            S.op(e, lambda: S.eng[e].tensor_tensor(out=o_ap, in0=a_ap, in1=b_ap, op=op),
                 reads=[aT, bT], writes=[oT])

        def ts(oT, o_ap, aT, a_ap, s1, op0, s2=None, op1=None, rd=(), e="dve"):
            def f():
                if op1 is None:
                    return S.eng[e].tensor_scalar(out=o_ap, in0=a_ap, scalar1=s1, scalar2=None, op0=op0)
                return S.eng[e].tensor_scalar(out=o_ap, in0=a_ap, scalar1=s1, scalar2=s2, op0=op0, op1=op1)
            S.op(e, f, reads=[aT, *rd], writes=[oT])

        def stt(oT, o_ap, aT, a_ap, sc, bT, b_ap, op0, op1, rd=()):
            S.op("dve", lambda: nc.vector.scalar_tensor_tensor(out=o_ap, in0=a_ap, scalar=sc, in1=b_ap, op0=op0, op1=op1),
                 reads=[aT, bT, *rd], writes=[oT])

        def cp(e, oT, o_ap, iT, i_ap):
            if e == "act":
                S.op("act", lambda: nc.scalar.copy(out=o_ap, in_=i_ap), reads=[iT], writes=[oT])
            else:
                S.op("dve", lambda: nc.vector.tensor_copy(out=o_ap, in_=i_ap), reads=[iT], writes=[oT])

        def recip(oT, o_ap, iT, i_ap):
            S.op("dve", lambda: nc.vector.reciprocal(out=o_ap, in_=i_ap), reads=[iT], writes=[oT])

        def memset(oT, o_ap, v):
            S.op("dve", lambda: nc.vector.memset(o_ap, v), writes=[oT])

        def col(c):
            return CC[:, c:c + 1]

        S.dma("sp", CC[:], vecs_d, writes=[CC])
        S.dma("sp", CF[:], cf_d, writes=[CF])
        S.dma_fresh(es, "pool", CBt[:], cb_d, writes=[CBt])
        S.dma_fresh(es, "pool", GKW[:], gkw_d, writes=[GKW])
        S.dma_fresh(es, "pool", W2T[:], w2_d, writes=[W2T])
        S.dma_fresh(es, "pool", A2T[32:64, :], a2_d, writes=[A2T])
        S.dma_fresh(es, "pool", G2T[:], g2_d, writes=[G2T])
        ts(CC, CC[:, C_NGKB:C_NGKB + 2], CC, CC[:, C_GKB:C_GKB + 2], -1.0, ALU.mult)
        ts(CC, CC[:, C_1MKA:C_1MKA + 4], CC, CC[:, C_KA:C_KA + 4], -1.0, ALU.mult, 1.0, ALU.add)

        def mk_scratch(name, w_ap, rows, cols, by_rows=False):
            sc = nc.dram_tensor(name, [rows, cols], BF16, kind="Internal").ap()
            chunks = []
            if by_rows:
                for r in range(0, rows, 1024):
                    chunks.append((r, min(r + 1024, rows), 0, cols))
            else:
                for c in range(0, cols, 1024):
                    chunks.append((0, rows, c, min(c + 1024, cols)))
            return {"sc": sc, "src": w_ap, "chunks": [(ch, T(None, name)) for ch in chunks]}

        WS = {
            "in": mk_scratch("wb_in", w_in, 1024, 3248),
            "out": mk_scratch("wb_out", w_out, 1024, 1024),
            "ff1": mk_scratch("wb_ff1", w_ff1, 1024, 4096),
            "ff2": mk_scratch("wb_ff2", w_ff2, 4096, 1024, by_rows=True),
            "pg": mk_scratch("wb_pg", w_pg, 1024, 1024),
            "pp": mk_scratch("wb_pp", w_pp, 256, 1024),
        }

        def convert(key, idxs=None):
            w = WS[key]
            for ci, ((r0_, r1_, c0_, c1_), tch) in enumerate(w["chunks"]):
                if idxs is not None and ci not in idxs:
                    continue
                S.dma_fresh(es, "pool", w["sc"][r0_:r1_, c0_:c1_], w["src"][r0_:r1_, c0_:c1_], writes=[tch])

        convert("in")
        convert("out")
        for g_ in range(4):
            convert("ff1", [g_])
            convert("ff2", [g_])
        convert("pg")
        convert("pp")

        def unit(key, r0, kc, c0, ncol):
            w = WS[key]
            deps = [tch for ((a0_, a1_, b0_, b1_), tch) in w["chunks"]
                    if a0_ < r0 + kc * 128 and r0 < a1_ and b0_ < c0 + ncol and c0 < b1_]
            return (w["sc"], r0, kc, c0, ncol, deps)

        units = []
        for b in range(NBLK):
            units.append(unit("in", 0, 8, 0, 512))
            units.append(unit("in", 0, 8, 512, 512))
            units.append(unit("in", 0, 8, 1024, 512))
            units.append(unit("in", 0, 8, 1536, 16))
            units.append(unit("in", 0, 8, 1552, 512))
            units.append(unit("in", 0, 8, 2064, 512))
            units.append(unit("in", 0, 8, 2576, 512))
            units.append(unit("in", 0, 8, 3088, 160))
            for hf in range(2):
                units.append(unit("out", 0, 8, hf * 512, 512))
            for g in range(4):
                for hf in range(2):
                    units.append(unit("ff1", 0, 8, g * 1024 + hf * 512, 512))
                for hf in range(2):
                    units.append(unit("ff2", g * 1024, 8, hf * 512, 512))
            for hf in range(2):
                units.append(unit("pg", 0, 8, hf * 512, 512))
            for hf in range(2):
                units.append(unit("pp", 0, 2, hf * 512, 512))
        ustate = {"issued": 0, "used": 0}

        def issue_unit():
            i = ustate["issued"]
            if i >= len(units):
                return
            w_ap, r0, kc, c0, ncol, deps = units[i]
            wb = WB[i % NWB]
            S.dma("sp", WBF[i % NWB][:, 0:kc, 0:ncol], w_ap[r0:r0 + kc * 128, c0:c0 + ncol].rearrange("(k p) c -> p k c", p=128),
                  reads=deps, writes=wb[0:kc])
            ustate["issued"] += 1

        def next_unit():
            i = ustate["used"]
            while ustate["issued"] <= i:
                issue_unit()
            ustate["used"] += 1
            return WB[i % NWB]

        def prefetch():
            while ustate["issued"] < min(ustate["used"] + NWB - 1, len(units)):
                issue_unit()

        issue_unit()

        def ksplit(t):
            out_ = []
            for k in range(8):
                tk = T(t.t[:, k, :], t.name + "_k%d" % k)
                tk.w, tk.r = t.w, dict(t.r)
                out_.append(tk)
            return out_

        def dense8_kouter(rhsK):
            wb0 = next_unit()
            prefetch()
            wb1 = next_unit()
            bks = [bank() for _ in range(8)]
            for k in range(8):
                for j in range(8):
                    wb = wb0 if j < 4 else wb1
                    mm(bks[j], bks[j][:], wb[k], wb[k][:, (j % 4) * 128:(j % 4 + 1) * 128], rhsK[k], rhsK[k][:, :],
                       start=(k == 0), stop=(k == 7), sig=(k == 7))
            prefetch()
            return bks

        def rmsnorm_gen(Hl, SQ, RS, gcol, dst, dst_ap_fn, n_feat=1024.0):
            bk = bank()
            for k in range(8):
                sq = SQ[k % 2]
                act(AF.Square, sq, sq[:], Hl[k], Hl[k][:])
                mm(bk, bk[:], CBt, ONES, sq, sq[:], start=(k == 0), stop=(k == 7), sig=True)
                if k % 2 == 1:
                    yield
            act(AF.Ln, RS, RS[:], bk, bk[:], scale=1.0 / n_feat, bias=EPS[:, 0:1], rd=[EPS])
            act(AF.Exp, RS, RS[:], RS, RS[:], scale=-0.5)
            yield
            for k in range(8):
                if isinstance(dst, list):
                    stt(dst[k], dst[k][:, :], Hl[k], Hl[k][:], col(gcol + k), RS, RS[:], ALU.mult, ALU.mult, rd=[CC])
                else:
                    stt(dst, dst_ap_fn(k), Hl[k], Hl[k][:], col(gcol + k), RS, RS[:], ALU.mult, ALU.mult, rd=[CC])
                if k % 2 == 1:
                    yield

        def rmsnorm(ph, gcol, dst, dst_ap_fn, n_feat=1024.0):
            SQ = [ph.sb("nsq%d" % i, [128, NB], BF16) for i in range(2)]
            RS = ph.sb("nrs", [128, NB], F32)
            for _ in rmsnorm_gen(hcur[0], SQ, RS, gcol, dst, dst_ap_fn, n_feat):
                pass

        def prologue(b):
            Hl = Hs[b % 2]
            ht = HT[b % 2]
            pt = PTs[b % 2]
            for tq in range(4):
                xs = XS[tq % 2]
                r0 = (tq // 2) * (TOK // 2) + b * (NB // 2) + (tq % 2) * 128
                S.dma("sp", xs[:], x[r0:r0 + 128, :], writes=[xs])
                for half in range(2):
                    bk = bank()
                    for kk in range(4):
                        k = half * 4 + kk
                        tr(bk, bk[:, kk * 128:(kk + 1) * 128], xs, xs[:, k * 128:(k + 1) * 128], CF, IDF, sig=(kk == 3))
                    e = "act" if half else "dve"
                    o_ap = ht[:, half * 4:half * 4 + 4, tq * 128:(tq + 1) * 128]
                    i_ap = bk[:, :].rearrange("p (k t) -> p k t", t=128)
                    if e == "act":
                        S.op("act", lambda: nc.scalar.copy(out=o_ap, in_=i_ap), reads=[bk], writes=Hl[half * 4:half * 4 + 4])
                    else:
                        S.op("dve", lambda: nc.vector.tensor_copy(out=o_ap, in_=i_ap), reads=[bk], writes=Hl[half * 4:half * 4 + 4])
                    yield
            for tq in range(4):
                pst = PSt[tq % 2]
                r0 = (tq // 2) * (TOK // 2) + b * (NB // 2) + (tq % 2) * 128
                S.dma("sp", pst[:], pin[r0:r0 + 128, :], writes=[pst])
                bk = bank()
                for c2 in range(2):
                    tr(bk, bk[:, c2 * 128:(c2 + 1) * 128], pst, pst[:, c2 * 128:(c2 + 1) * 128], CF, IDF, sig=(c2 == 1))
                cp("act", pt, pt[:, :, tq * 128:(tq + 1) * 128], bk, bk[:, 0:256].rearrange("p (c t) -> p c t", t=128))
                yield
            with Phase(S, nc) as pp:
                NSQ = [pp.sb("NSQ%d" % i, [128, NB], BF16) for i in range(2)]
                NRS = pp.sb("NRS", [128, NB], F32)
                for _ in rmsnorm_gen(Hl, NSQ, NRS, C_LNMIX, XN1K, None):
                    yield

        EPS = sb("EPS", [128, 4], F32)
        memset(EPS, EPS[:, 0:1], 1e-6)
        memset(EPS, EPS[:, 1:2], 64e-5)
        memset(EPS, EPS[:, 2:3], 1.0)

        out_dmas = []

        class Sub:
            def __init__(self, p):
                self.p = p

            def __enter__(self):
                return self

            def sb(self, *a):
                return self.p.sb(*a)

            def __exit__(self, *a):
                return False

        tailp = [None]

        def get_tail():
            if tailp[0] is None:
                tailp[0] = Phase(S, nc)
                tailp[0].__enter__()
            return tailp[0]

        pro = [prologue(0)]
        for b in range(nblk):
            t0 = b * NB
            if b == 0:
                memset(GC, GC[:], 0.0)
                memset(RC, RC[:], 0.0)
                for i_ in range(2):
                    memset(GST[i_], GST[i_][:], 0.0)
                    memset(GSTB[i_], GSTB[i_][:], 0.0)
                    memset(TS[i_], TS[i_][:], 0.0)
                    memset(TSB[i_][0], TSB[i_][0][:], 0.0)
                    memset(TSB[i_][1], TSB[i_][1][:], 0.0)
            try:
                H = Hs[b % 2]
                hcur[0] = H
                PT = PTs[b % 2]
                XNK = XN1K
                if pro[0] is None:
                    pro[0] = prologue(b)
                for _ in pro[0]:
                    pass
                pro[0] = None

                with Phase(S, nc) as mx:
                    OTB = mx.sb("OTB", [128, 8, NB], BF16)

                    chk(1)
                    with Phase(S, nc) as ph:
                        ZC = [ph.sb("zc%d" % j, [128, 2, 4 + NB // 2], BF16) for j in range(8)]
                        SG = [ph.sb("sg%d" % j, [128, NB], F32) for j in range(4)]
                        GL = ph.sb("gl", [16, NB], BF16)
                        DG = ph.sb("dg", [128, 32, 128], BF16)
                        for kk in range(4):
                            for j in range(8):
                                ts(DG, DG[:, kk * 8 + j, :], CBt, IDB, col(C_CONV + kk * 8 + j), ALU.mult, rd=[CC])
                        QS = [ph.sb("qs%d" % j, [128, NB], F32) for j in range(2)]
                        KS = [ph.sb("ks%d" % j, [128, NB], F32) for j in range(2)]
                        QT = [ph.sb("qt%d" % j, [128, NB], BF16) for j in range(2)]
                        KT = [ph.sb("kt%d" % j, [128, NB], BF16) for j in range(2)]
                        VBF = [ph.sb("vbf%d" % j, [128, NB], BF16) for j in range(4)]
                        EB = [ph.sb("eb%d" % j, [128, NB], F32) for j in range(2)]
                        ENB = [ph.sb("enb%d" % j, [128, NB], F32) for j in range(2)]
                        E1 = ph.sb("e1", [128, NB], F32)
                        KVT = [ph.sb("kvt%d" % j, [128, 768], BF16) for j in range(4)]
                        AT = [ph.sb("at%d" % j, [128, 256], BF16) for j in range(4)]
                        OTA = ph.sb("ota", [128, 4, NB], F32)
                        TMP = [ph.sb("tmpg%d" % i_, [128, 2, 128], F32) for i_ in range(2)]
                        SQ = ph.sb("sqg", [128, NB], BF16)
                        RS = ph.sb("rsg", [128, NB], F32)
                        T1 = ph.sb("t1g", [128, NB], F32)

                        bks8 = dense8_kouter(XNK)
                        for j in range(8):
                            bk = bks8[j]
                            cp("dve", ZC[j], ZC[j][:, :, 0:4], GC, GC[:, j, :, :])
                            cp("act", ZC[j], ZC[j][:, :, 4:4 + NB // 2], bk, bk[:, :].rearrange("p (s t) -> p s t", s=2))
                            cp("dve", GC, GC[:, j, :, :], ZC[j], ZC[j][:, :, NB // 2:NB // 2 + 4])
                        wb = next_unit()
                        prefetch()
                        for j in range(4):
                            bk = bank()
                            for k in range(8):
                                mm(bk, bk[:], wb[k], wb[k][:, (j % 4) * 128:(j % 4 + 1) * 128], XNK[k], XNK[k][:, :], start=(k == 0), stop=(k == 7), sig=(k == 7))
                            act(AF.Silu, SG[j], SG[j][:], bk, bk[:])
                        wb = next_unit()
                        prefetch()
                        bk = bank()
                        for k in range(8):
                            mm(bk, bk[0:16, :], wb[k], wb[k][:, 0:16], XNK[k], XNK[k][:, :], start=(k == 0), stop=(k == 7), sig=(k == 7))
                        cp("act", GL, GL[:], bk, bk[0:16, :])
                        sub(1)
                        for j in range(8):
                            z = ZC[j]
                            bk = bank()
                            for kk in range(4):
                                mm(bk, bk[:, :].rearrange("p (s t) -> p s t", s=2), DG, DG[:, kk * 8 + j, :], z, z[:, :, 1 + kk:1 + kk + NB // 2],
                                   start=(kk == 0), stop=(kk == 3), sig=(kk == 3))
                            if j < 2:
                                act(AF.Silu, QS[j], QS[j][:], bk, bk[:])
                            elif j < 4:
                                act(AF.Silu, KS[j - 2], KS[j - 2][:], bk, bk[:])
                            else:
                                act(AF.Silu, VBF[j - 4], VBF[j - 4][:], bk, bk[:])
                        sub(2)
                        for m in range(2):
                            bk = bank()
                            mm(bk, bk[:], GKW, GKW[0:16, m * 128:(m + 1) * 128], GL, GL[0:16, :], sig=True)
                            act(AF.Exp, E1, E1[:], bk, bk[:], scale=-1.0, bias=col(C_NGKB + m), rd=[CC])
                            act(AF.Ln, E1, E1[:], E1, E1[:], bias=EPS[:, 2:3], rd=[EPS])
                            S.op("dve", lambda: nc.vector.tensor_tensor_scan(out=EB[m][:], data0=MSCAN, data1=E1[:], initial=0.0, op0=ALU.mult, op1=ALU.add),
                                 reads=[CF, E1], writes=[EB[m]])
                            act(AF.Exp, ENB[m], ENB[m][:], EB[m], EB[m][:], scale=1.0 / 16.0)
                            act(AF.Exp, EB[m], EB[m][:], EB[m], EB[m][:], scale=-1.0 / 16.0)
                            stt(QT[m], QT[m][:], QS[m], QS[m][:], 0.125, EB[m], EB[m][:], ALU.mult, ALU.mult)
                            tt(KT[m], KT[m][:], KS[m], KS[m][:], ENB[m], ENB[m][:], ALU.mult)
                        sub(3)
                        for c4 in range(4):
                            bk = bank()
                            bb = bf(bk)
                            for m in range(2):
                                tr(bk, bb[:, m * 128:(m + 1) * 128], KT[m], KT[m][:, c4 * 128:(c4 + 1) * 128], CBt, IDB)
                            for h in range(4):
                                tr(bk, bb[:, 256 + h * 128:256 + (h + 1) * 128], VBF[h], VBF[h][:, c4 * 128:(c4 + 1) * 128], CBt, IDB, sig=(h == 3))
                            cp("act" if c4 % 2 else "dve", KVT[c4], KVT[c4][:], bk, bb[:, 0:768])
                        sub(4)
                        for c4 in range(4):
                            bsc = [bank(), bank()]
                            for hp in range(2):
                                hb = hp * 64
                                for m in range(2):
                                    for par in range(2):
                                        c = 2 * c4 + par
                                        cb = par * 64
                                        mm(bsc[hp], bsc[hp][cb:cb + 64, m * 64:(m + 1) * 64], KT[m], KT[m][hb:hb + 64, c * 64:(c + 1) * 64],
                                           QT[m], QT[m][hb:hb + 64, c * 64:(c + 1) * 64], sig=(m == 1 and par == 1))
                            for hp in range(2):
                                at_v = AT[c4][:, :].rearrange("p (m q i) -> p m q i", q=2, i=64)[:, :, hp, :]
                                tt(AT[c4], at_v, bsc[hp], bsc[hp][:, 0:128].rearrange("p (m i) -> p m i", i=64),
                                   CBt, CBt[:, CB_MG:CB_MG + 128].rearrange("p (m i) -> p m i", i=64), ALU.mult)
                        sub(5)
                        for cc in range(4):
                            cs = (cc, 4 + cc)
                            bos, bkvs = {}, {}
                            for si, c in enumerate(cs):
                                c4, par = divmod(c, 2)
                                cb = par * 64
                                bo, boa = bank(), bank()
                                same = [h for h in range(4) if (h % 2) * 64 == cb]
                                diff = [h for h in range(4) if (h % 2) * 64 != cb]
                                for h in same + diff:
                                    m, hb = h // 2, (h % 2) * 64
                                    mm(bo, bo[:, h * 64:(h + 1) * 64], KVT[c4], KVT[c4][cb:cb + 64, 256 + h * 128:256 + (h + 1) * 128],
                                       AT[c4], AT[c4][cb:cb + 64, h * 64:(h + 1) * 64], start=True, stop=(h in diff), sig=(h == diff[-1]))
                                    if h in same:
                                        mm(bo, bo[:, h * 64:(h + 1) * 64], GSTB[si], GSTB[si][hb:hb + 64, m, :],
                                           QT[m], QT[m][hb:hb + 64, c * 64:(c + 1) * 64], start=False, stop=True)
                                for h in diff:
                                    m, hb = h // 2, (h % 2) * 64
                                    mm(boa, boa[:, h * 64:(h + 1) * 64], GSTB[si], GSTB[si][hb:hb + 64, m, :],
                                       QT[m], QT[m][hb:hb + 64, c * 64:(c + 1) * 64], start=True, stop=True, sig=(h == diff[-1]))
                                bos[si] = (bo, boa)
                                bkv = bank()
                                for h in range(4):
                                    m, hb = h // 2, (h % 2) * 64
                                    mm(bkv, bkv[hb:hb + 64, m * 128:(m + 1) * 128], KVT[c4], KVT[c4][cb:cb + 64, h * 64:(h + 1) * 64],
                                       KVT[c4], KVT[c4][cb:cb + 64, 256 + h * 128:256 + (h + 1) * 128], sig=(h == 3))
                                bkvs[si] = bkv
                            for si, c in enumerate(cs):
                                (bo, boa), bkv = bos[si], bkvs[si]
                                hpd = 1 - (c % 2)
                                cp("act", OTA, OTA[:, :, c * 64:(c + 1) * 64], bo, bo[:, 0:256].rearrange("p (h i) -> p h i", i=64))
                                o_v = OTA[:, :, c * 64:(c + 1) * 64].rearrange("p (m q) i -> p m q i", q=2)[:, :, hpd, :]
                                a_v = boa[:, 0:256].rearrange("p (m q i) -> p m q i", q=2, i=64)[:, :, hpd, :]
                                tt(OTA, o_v, boa, a_v, OTA, o_v, ALU.add)
                                tt(TMP[si], TMP[si][:], GST[si], GST[si][:], bkv, bkv[:, 0:256].rearrange("p (m e) -> p m e", e=128), ALU.add)
                                for m in range(2):
                                    ts(GST[si], GST[si][:, m, :], TMP[si], TMP[si][:, m, :], EB[m][:, c * 64 + 63:c * 64 + 64], ALU.mult, rd=[EB[m]])
                                cp("act", GSTB[si], GSTB[si][:], GST[si], GST[si][:])
                        sub(6)
                        def gdrain(*gens):
                            gens = list(gens)
                            while gens:
                                for g in list(gens):
                                    try:
                                        next(g)
                                    except StopIteration:
                                        gens.remove(g)

                        nSQ = [QT[0], QT[1], KT[0], KT[1]]
                        nRS = [QS[0], QS[1], KS[0], KS[1]]
                        nT1 = [EB[0], EB[1], ENB[0], ENB[1]]

                        def gnorm(h):
                            SQh, RSh, T1h = nSQ[h], nRS[h], nT1[h]
                            act(AF.Square, SQh, SQh[:], OTA, OTA[:, h, :])
                            yield
                            bk = bank()
                            mm(bk, bk[:], CBt, ONES, SQh, SQh[:], sig=True)
                            yield
                            act(AF.Ln, RSh, RSh[:], bk, bk[:], scale=1.0 / 128.0, bias=EPS[:, 0:1], rd=[EPS])
                            yield
                            act(AF.Exp, RSh, RSh[:], RSh, RSh[:], scale=-0.5)
                            yield
                            stt(T1h, T1h[:], OTA, OTA[:, h, :], col(C_NG), RSh, RSh[:], ALU.mult, ALU.mult, rd=[CC])
                            yield
                            tt(OTB, OTB[:, h, :], T1h, T1h[:], SG[h], SG[h][:], ALU.mult)

                        gdrain(gnorm(0), gnorm(1), gnorm(2), gnorm(3))

                    try:
                        chk(2)
                        with Phase(S, nc) as ph:
                            GG = [ph.sb("gg%d" % j, [128, NB], F32) for j in range(4)]
                            BON = [ph.sb("bon%d" % j, [128, NB], F32) for j in range(4)]
                            AR = [ph.sb("ar%d" % j, [128, 2, NB], BF16) for j in range(4)]
                            BK = [ph.sb("bk%d" % j, [128, 2, NB], BF16) for j in range(4)]
                            VB = [ph.sb("vb%d" % j, [128, NB], BF16) for j in range(4)]
                            ELC = ph.sb("elc", [128, 4, 8], F32)
                            p2 = Phase(S, nc)
                            p2.__enter__()
                            ZR = [p2.sb("zr%d" % j, [128, 2, 4 + NB // 2], F32) for j in range(2)]
                            ZS = [p2.sb("zs%d" % j, [128, NB], F32) for j in range(14)]
                            LRB = p2.sb("lrb", [64, NB], BF16)
                            GLB = p2.sb("glb", [96, NB], BF16)
                            D = p2.sb("dd", [128, NB], F32)


                            def zchunk(wb, c0, npart, zi):
                                bk = bank()
                                for k in range(8):
                                    mm(bk, bk[0:npart, :], wb[k], wb[k][:, c0:c0 + npart], XNK[k], XNK[k][:, :], start=(k == 0), stop=(k == 7), sig=(k == 7))
                                z = ZR[zi % 2]
                                hn = NB // 2
                                cp("dve", z, z[0:npart, :, 0:4], RC, RC[0:npart, zi, :, :])
                                cp("act", z, z[0:npart, :, 4:4 + hn], bk, bk[0:npart, :].rearrange("p (s t) -> p s t", s=2))
                                cp("dve", RC, RC[0:npart, zi, :, :], z, z[0:npart, :, hn:hn + 4])
                                mucol = C_MU + zi if zi < 12 else (C_MULR if zi == 12 else C_MUG)
                                D3 = D[0:npart, :].rearrange("p (s t) -> p s t", s=2)
                                ZS3 = ZS[zi][0:npart, :].rearrange("p (s t) -> p s t", s=2)
                                tt(D, D3, z, z[0:npart, :, 3:3 + hn], z, z[0:npart, :, 4:4 + hn], ALU.subtract)
                                stt(ZS[zi], ZS3, D, D3, CC[0:npart, mucol:mucol + 1], z, z[0:npart, :, 4:4 + hn], ALU.mult, ALU.add, rd=[CC])

                            for j in range(12):
                                if j % 4 == 0:
                                    wb = next_unit()
                                    prefetch()
                                zchunk(wb, (j % 4) * 128, 128, j)
                            wb = next_unit()
                            prefetch()
                            zchunk(wb, 0, 64, 12)
                            zchunk(wb, 64, 96, 13)
                            act(AF.Tanh, LRB, LRB[0:32, :], ZS[12], ZS[12][0:32, :])
                            cp("act", LRB, LRB[32:64, :], ZS[12], ZS[12][32:64, :])
                            act(AF.Sigmoid, GLB, GLB[:], ZS[13], ZS[13][0:96, :])

                            if True:
                                scr = [{nm: p2.sb("%s%d" % (nm, q), [128, NB], (BF16 if nm == "sqr" else F32))
                                        for nm in ("sgm", "aa", "lp", "el", "enl", "elm", "kkr", "sqr", "rn", "kk")} for q in range(2)]
                                def bind(m):
                                    sc_ = scr[m % 2]
                                    return (sc_["sgm"], sc_["aa"], sc_["lp"], sc_["el"], sc_["enl"], sc_["elm"], sc_["kkr"], sc_["sqr"], sc_["rn"], sc_["kk"])

                                def stage_a(m):
                                    SGm, AA, LP, EL, ENL, ELM, KKR, SQ, RN, KK = bind(m)
                                    zr, zk, zv = ZS[m], ZS[4 + m], ZS[8 + m]
                                    bk = bank()
                                    mm(bk, bk[:], W2T, W2T[0:32, m * 128:(m + 1) * 128], LRB, LRB[0:32, :], sig=True)
                                    yield
                                    act(AF.Sigmoid, SGm, SGm[:], bk, bk[:], bias=col(C_W0 + m), rd=[CC])
                                    yield
                                    bk = bank()
                                    mm(bk, bk[:], A2T, A2T[32:64, m * 128:(m + 1) * 128], LRB, LRB[32:64, :], sig=True)
                                    yield
                                    act(AF.Sigmoid, AA, AA[:], bk, bk[:], bias=col(C_A0 + m), rd=[CC])
                                    yield
                                    bk = bank()
                                    mm(bk, bk[:], G2T, G2T[0:96, m * 128:(m + 1) * 128], GLB, GLB[0:96, :], sig=True)
                                    yield
                                    cp("act", GG[m], GG[m][:], bk, bk[:])
                                    yield
                                    S.op("dve", lambda: nc.vector.tensor_tensor_scan(out=LP[:], data0=MSCAN, data1=SGm[:], initial=0.0, op0=ALU.mult, op1=ALU.add),
                                         reads=[CF, SGm], writes=[LP])
                                    act(AF.Exp, EL, EL[:], LP, LP[:], scale=-CDEC)
                                    yield
                                    act(AF.Exp, ENL, ENL[:], LP, LP[:], scale=CDEC)
                                    yield
                                    tt(LP, LP[:], LP, LP[:], SGm, SGm[:], ALU.subtract, e=POOL_E)
                                    yield
                                    act(AF.Exp, ELM, ELM[:], LP, LP[:], scale=-CDEC)
                                    yield
                                    cp("dve", ELC, ELC[:, m, :], EL, EL[:, :].rearrange("p (c t) -> p c t", t=64)[:, :, 63])
                                    yield
                                    ts(KKR, KKR[:], zk, zk[:], col(C_KK + m), ALU.mult, rd=[CC])
                                    yield
                                    act(AF.Square, SQ, SQ[:], KKR, KKR[:])
                                    yield
                                    bk = bank()
                                    mm(bk, bk[:], CBt, BD, SQ, SQ[:], sig=True)
                                    yield
                                    ts(RN, RN[:], bk, bk[:], 1e-19, ALU.max)
                                    yield
                                    act(AF.Ln, RN, RN[:], RN, RN[:])
                                    yield
                                    act(AF.Exp, RN, RN[:], RN, RN[:], scale=-0.5)
                                    yield

                                def stage_b(m):
                                    SGm, AA, LP, EL, ENL, ELM, KKR, SQ, RN, KK = bind(m)
                                    TA = KKR
                                    KF = RN
                                    TB = TA
                                    RKB = SQ
                                    zr, zk, zv = ZS[m], ZS[4 + m], ZS[8 + m]
                                    tt(KK, KK[:], KKR, KKR[:], RN, RN[:], ALU.mult)
                                    yield
                                    ts(TA, TA[:], AA, AA[:], col(C_KA + m), ALU.mult, col(C_1MKA + m), ALU.add, rd=[CC])
                                    yield
                                    tt(KF, KF[:], zk, zk[:], TA, TA[:], ALU.mult, e=POOL_E)
                                    yield
                                    stt(AR[m], AR[m][:, 0, :], KK, KK[:], -1.0, ELM, ELM[:], ALU.mult, ALU.mult)
                                    yield
                                    tt(AR[m], AR[m][:, 1, :], zr, zr[:], EL, EL[:], ALU.mult, e=POOL_E)
                                    yield
                                    tt(TB, TB[:], KK, KK[:], AA, AA[:], ALU.mult)
                                    yield
                                    tt(BK[m], BK[m][:, 0, :], TB, TB[:], ENL, ENL[:], ALU.mult)
                                    yield
                                    tt(BK[m], BK[m][:, 1, :], KF, KF[:], ENL, ENL[:], ALU.mult, e=POOL_E)
                                    yield
                                    cp("act", VB[m], VB[m][:], zv, zv[:])
                                    yield
                                    stt(RKB, RKB[:], zr, zr[:], col(C_RK + m), KF, KF[:], ALU.mult, ALU.mult, rd=[CC])
                                    yield
                                    bk = bank()
                                    mm(bk, bk[:], CBt, BD, RKB, RKB[:], sig=True)
                                    yield
                                    tt(BON[m], BON[m][:], bk, bk[:], zv, zv[:], ALU.mult)
                                    yield

                                def drain(*gens):
                                    gens = [g for g in gens if g is not None]
                                    while gens:
                                        for g in list(gens):
                                            try:
                                                next(g)
                                            except StopIteration:
                                                gens.remove(g)

                                drain(stage_a(0))
                                for m in range(4):
                                    drain(stage_a(m + 1) if m + 1 < 4 else None, stage_b(m))

                            p2.__exit__(None, None, None)
                            with Phase(S, nc) as p3:
                                YT = p3.sb("yt", [128, 4, NB], F32)
                                BKT = [p3.sb("bkt%d" % j, [128, 1024], BF16) for j in range(4)]
                                VT = [p3.sb("vt%d" % j, [128, NB], BF16) for j in range(4)]
                                CMB = [p3.sb("cmb%d" % j, [128, 8, 128], BF16) for j in range(4)]
                                CMK = [p3.sb("cmk%d" % j, [128, 8, 128], BF16) for j in range(4)]
                                PV = [p3.sb("pv%d" % j, [128, 8, 2, 64], BF16) for j in range(4)]
                                QQ = [p3.sb("qq%d" % j, [128, 8, 64], BF16) for j in range(4)]
                                X1 = [p3.sb("x1_%d" % i_, [128, 8, 64], BF16) for i_ in range(2)]
                                U1 = [p3.sb("u1_%d" % i_, [128, 8, 64], BF16) for i_ in range(2)]
                                TMPR = [p3.sb("tmpr%d" % i_, [128, 4, 64], F32) for i_ in range(2)]
                                I8v = CBt[:, CB_I8:CB_I8 + 512].rearrange("p (h t) -> p h t", t=64)
                                MQv = CBt[:, CB_MQ:CB_MQ + 512].rearrange("p (h t) -> p h t", t=64)
                                MRWv = CBt[:, CB_MRW:CB_MRW + 512].rearrange("p (h t) -> p h t", t=128)

                                def v3(bk, w):
                                    return bk[:, :].rearrange("p (h t) -> p h t", t=w)

                                for c4 in range(4):
                                    bk = bank()
                                    bb = bf(bk)
                                    for m in range(4):
                                        tr(bk, bb[:, m * 128:(m + 1) * 128], BK[m], BK[m][:, 0, c4 * 128:(c4 + 1) * 128], CBt, IDB)
                                    for m in range(4):
                                        tr(bk, bb[:, 512 + m * 128:512 + (m + 1) * 128], BK[m], BK[m][:, 1, c4 * 128:(c4 + 1) * 128], CBt, IDB, sig=(m == 3))
                                    cp("act", BKT[c4], BKT[c4][:], bk, bb[:, :])
                                    bk = bank()
                                    bb = bf(bk)
                                    for m in range(4):
                                        tr(bk, bb[:, m * 128:(m + 1) * 128], VB[m], VB[m][:, c4 * 128:(c4 + 1) * 128], CBt, IDB, sig=(m == 3))
                                    cp("dve", VT[c4], VT[c4][:], bk, bb[:, 0:512])
                                for c4 in range(4):
                                    bB = [bank(), bank()]
                                    bK = [bank(), bank()]
                                    bQ = [bank(), bank()]
                                    for hp in range(2):
                                        hb = hp * 64
                                        for m in range(4):
                                            h = 2 * m + hp
                                            for par in range(2):
                                                c = 2 * c4 + par
                                                cb = par * 64
                                                last = (m == 3 and par == 1)
                                                rhs_ar = AR[m][hb:hb + 64, :, c * 64:(c + 1) * 64]
                                                mm(bB[hp], bB[hp][cb:cb + 64, m * 128:(m + 1) * 128], BK[m], BK[m][hb:hb + 64, 0, c * 64:(c + 1) * 64], AR[m], rhs_ar, sig=last)
                                                mm(bK[hp], bK[hp][cb:cb + 64, m * 128:(m + 1) * 128], BK[m], BK[m][hb:hb + 64, 1, c * 64:(c + 1) * 64], AR[m], rhs_ar, sig=last)
                                                mm(bQ[hp], bQ[hp][cb:cb + 64, m * 64:(m + 1) * 64], AR[m], AR[m][hb:hb + 64, 0, c * 64:(c + 1) * 64],
                                                   BK[m], BK[m][hb:hb + 64, 0, c * 64:(c + 1) * 64], sig=last)
                                    for hp in range(2):
                                        cmb_v = CMB[c4][:, :, :].rearrange("p (m q) t -> p m q t", q=2)[:, :, hp, :]
                                        cmk_v = CMK[c4][:, :, :].rearrange("p (m q) t -> p m q t", q=2)[:, :, hp, :]
                                        qq_v = QQ[c4][:, :, :].rearrange("p (m q) t -> p m q t", q=2)[:, :, hp, :]
                                        tt(CMB[c4], cmb_v, bB[hp], v3(bB[hp], 128), CBt, MRWv, ALU.mult)
                                        tt(CMK[c4], cmk_v, bK[hp], v3(bK[hp], 128), CBt, MRWv, ALU.mult)
                                        tt(QQ[c4], qq_v, bQ[hp], bQ[hp][:, 0:256].rearrange("p (m t) -> p m t", t=64), CBt, MQv[:, 0:4, :], ALU.mult)
                                    cp("act", PV[c4], PV[c4][:, :, 0, :], CMB[c4], CMB[c4][:, :, 0:64])
                                    cp("act", PV[c4], PV[c4][:, :, 1, :], CBt, I8v)
                                for kd in range(6):
                                    for half in range(2):
                                        c4s = (2 * half, 2 * half + 1)
                                        bks = {}
                                        for c4 in c4s:
                                            ba, bb_ = bank(), (bank() if kd < 5 else None)
                                            b2_ = bank() if kd < 5 else None
                                            for par in range(2):
                                                cb = par * 64
                                                for h in range(8):
                                                    last = (par == 1 and h % 4 == 3)
                                                    Qk = QQ[c4][cb:cb + 64, h, :]
                                                    if kd < 5:
                                                        ob = ba if h < 4 else bb_
                                                        hl = h % 4
                                                        mm(ob, ob[cb:cb + 64, hl * 128:(hl + 1) * 128], QQ[c4], Qk, PV[c4], PV[c4][cb:cb + 64, h, :, :], sig=last, force=False)
                                                        mm(b2_, b2_[cb:cb + 64, h * 64:(h + 1) * 64], PV[c4], PV[c4][cb:cb + 64, h, 0, :], QQ[c4], Qk, sig=(par == 1 and h == 7), force=False)
                                                    else:
                                                        mm(ba, ba[cb:cb + 64, h * 64:(h + 1) * 64], QQ[c4], Qk, PV[c4], PV[c4][cb:cb + 64, h, 1, :], sig=(par == 1 and h == 7), force=False)
                                            bks[c4] = (ba, bb_, b2_)
                                        for c4 in c4s:
                                            ba, bb_, b2_ = bks[c4]
                                            if kd < 5:
                                                for hh, bx_ in ((0, ba), (4, bb_)):
                                                    cp("act", PV[c4], PV[c4][:, hh:hh + 4, 0, :], bx_, v3(bx_, 128)[:, :, 0:64])
                                                    tt(PV[c4], PV[c4][:, hh:hh + 4, 1, :], bx_, v3(bx_, 128)[:, :, 64:128], PV[c4], PV[c4][:, hh:hh + 4, 1, :], ALU.add)
                                                cp("dve", QQ[c4], QQ[c4][:], b2_, v3(b2_, 64))
                                            else:
                                                tt(PV[c4], PV[c4][:, :, 1, :], ba, v3(ba, 64), PV[c4], PV[c4][:, :, 1, :], ALU.add)
                                for cc in range(4):
                                    cs = (cc, 4 + cc)
                                    q0, q1 = cc % 2, (cc + 1) % 2
                                    bxs, bus, bts, bys = {}, {}, {}, {}
                                    for si, c in enumerate(cs):
                                        c4, par = divmod(c, 2)
                                        cb = par * 64
                                        bx, bxa = bank(), bank()
                                        same = [h for h in range(8) if (h % 2) * 64 == cb]
                                        diff = [h for h in range(8) if (h % 2) * 64 != cb]
                                        for h in same:
                                            m, hb = h // 2, (h % 2) * 64
                                            mm(bx, bx[cb:cb + 64, h * 64:(h + 1) * 64], AR[m], AR[m][hb:hb + 64, 0, c * 64:(c + 1) * 64],
                                               TSB[si][q0], TSB[si][q0][hb:hb + 64, m, :], start=True, stop=False)
                                            mm(bx, bx[cb:cb + 64, h * 64:(h + 1) * 64], CMK[c4], CMK[c4][cb:cb + 64, h, 0:64],
                                               VT[c4], VT[c4][cb:cb + 64, h * 64:(h + 1) * 64], start=False, stop=True)
                                        for h in diff:
                                            mm(bx, bx[cb:cb + 64, h * 64:(h + 1) * 64], CMK[c4], CMK[c4][cb:cb + 64, h, 0:64],
                                               VT[c4], VT[c4][cb:cb + 64, h * 64:(h + 1) * 64], start=True, stop=True, sig=(h == diff[-1]))
                                        for h in diff:
                                            m, hb = h // 2, (h % 2) * 64
                                            mm(bxa, bxa[cb:cb + 64, h * 64:(h + 1) * 64], AR[m], AR[m][hb:hb + 64, 0, c * 64:(c + 1) * 64],
                                               TSB[si][q0], TSB[si][q0][hb:hb + 64, m, :], start=True, stop=True, sig=(h == diff[-1]))
                                        bxs[si] = (bx, bxa)
                                    for si, c in enumerate(cs):
                                        cb = (c % 2) * 64
                                        bx, bxa = bxs[si]
                                        hpd = 1 - (c % 2)
                                        cp("act", X1[si], X1[si][cb:cb + 64, :, :], bx, bx[cb:cb + 64, :].rearrange("p (h t) -> p h t", t=64))
                                        xv = X1[si][cb:cb + 64, :, :].rearrange("p (m q) t -> p m q t", q=2)[:, :, hpd, :]
                                        av = bxa[cb:cb + 64, :].rearrange("p (m q t) -> p m q t", q=2, t=64)[:, :, hpd, :]
                                        tt(X1[si], xv, bxa, av, X1[si], xv, ALU.add)
                                    for si, c in enumerate(cs):
                                        c4, par = divmod(c, 2)
                                        cb = par * 64
                                        bu = bank()
                                        for h in range(8):
                                            mm(bu, bu[cb:cb + 64, h * 64:(h + 1) * 64], PV[c4], PV[c4][cb:cb + 64, h, 1, :], X1[si], X1[si][cb:cb + 64, h, :], sig=(h == 7), force=False)
                                        bus[si] = bu
                                    for si, c in enumerate(cs):
                                        cb = (c % 2) * 64
                                        bu = bus[si]
                                        cp("dve", U1[si], U1[si][cb:cb + 64, :, :], bu, bu[cb:cb + 64, :].rearrange("p (h t) -> p h t", t=64))
                                    for si, c in enumerate(cs):
                                        c4, par = divmod(c, 2)
                                        cb = par * 64
                                        bt = bank()
                                        for h in range(8):
                                            m, hb = h // 2, (h % 2) * 64
                                            o_ap = bt[hb:hb + 64, m * 64:(m + 1) * 64]
                                            mm(bt, o_ap, BKT[c4], BKT[c4][cb:cb + 64, h * 64:(h + 1) * 64], U1[si], U1[si][cb:cb + 64, h, :], start=True, stop=False)
                                            mm(bt, o_ap, BKT[c4], BKT[c4][cb:cb + 64, 512 + h * 64:512 + (h + 1) * 64], VT[c4], VT[c4][cb:cb + 64, h * 64:(h + 1) * 64],
                                               start=False, stop=True, sig=(h == 7))
                                        bts[si] = bt
                                    for si, c in enumerate(cs):
                                        bt = bts[si]
                                        tt(TMPR[si], TMPR[si][:], TS[si], TS[si][:], bt, bt[:, 0:256].rearrange("p (m i) -> p m i", i=64), ALU.add)
                                        tt(TS[si], TS[si][:], TMPR[si], TMPR[si][:], ELC, ELC[:, :, c:c + 1].broadcast_to([128, 4, 64]), ALU.mult)
                                        cp("act", TSB[si][q1], TSB[si][q1][:], TS[si], TS[si][:])
                                    for si, c in enumerate(cs):
                                        c4, par = divmod(c, 2)
                                        cb = par * 64
                                        by, bya = bank(), bank()
                                        same = [h for h in range(8) if (h % 2) * 64 == cb]
                                        diff = [h for h in range(8) if (h % 2) * 64 != cb]
                                        for h in same + diff:
                                            m, hb = h // 2, (h % 2) * 64
                                            o_ap = by[hb:hb + 64, m * 64:(m + 1) * 64]
                                            if h in same:
                                                mm(by, o_ap, TSB[si][q0], TSB[si][q0][hb:hb + 64, m, :], AR[m], AR[m][hb:hb + 64, 1, c * 64:(c + 1) * 64], start=True, stop=False)
                                            mm(by, o_ap, U1[si], U1[si][cb:cb + 64, h, :], CMB[c4], CMB[c4][cb:cb + 64, h, 64:128], start=(h in diff), stop=False)
                                            mm(by, o_ap, VT[c4], VT[c4][cb:cb + 64, h * 64:(h + 1) * 64], CMK[c4], CMK[c4][cb:cb + 64, h, 64:128], start=False, stop=True, sig=(h == diff[-1]))
                                        for h in diff:
                                            m, hb = h // 2, (h % 2) * 64
                                            mm(bya, bya[hb:hb + 64, m * 64:(m + 1) * 64], TSB[si][q0], TSB[si][q0][hb:hb + 64, m, :], AR[m], AR[m][hb:hb + 64, 1, c * 64:(c + 1) * 64],
                                               start=True, stop=True, sig=(h == diff[-1]))
                                        bys[si] = (by, bya)
                                    for si, c in enumerate(cs):
                                        by, bya = bys[si]
                                        hd = (1 - (c % 2)) * 64
                                        cp("act", YT, YT[:, :, c * 64:(c + 1) * 64], by, by[:, 0:256].rearrange("p (m t) -> p m t", t=64))
                                        tt(YT, YT[hd:hd + 64, :, c * 64:(c + 1) * 64], bya, bya[hd:hd + 64, 0:256].rearrange("p (m t) -> p m t", t=64),
                                           YT, YT[hd:hd + 64, :, c * 64:(c + 1) * 64], ALU.add)

                                with Phase(S, nc) as p4:
                                    def alias(tl, dt):
                                        nt = T(tl.t[:].bitcast(dt), tl.name + "_al")
                                        nt.w, nt.r = tl.w, dict(tl.r)
                                        return nt
                                    def alias2(tl, ap, nm):
                                        nt = T(ap, nm)
                                        nt.w, nt.r = tl.w, dict(tl.r)
                                        return nt

                                    def gn(m):
                                        YB = VT[m]
                                        SQ = alias2(CMK[m], CMK[m].t[:, 0:4, :].rearrange("p a b -> p (a b)"), "sq4")
                                        YC = alias(BKT[m], F32)
                                        RS = alias2(CMB[m], CMB[m].t[:, :, :].rearrange("p a b -> p (a b)").bitcast(F32), "rs4")
                                        cp("act", YB, YB[:], YT, YT[:, m, :])
                                        yield
                                        bk = bank()
                                        mm(bk, bk[:], CBt, BD, YB, YB[:], sig=True)
                                        yield
                                        stt(YC, YC[:], bk, bk[:], -1.0 / 64.0, YT, YT[:, m, :], ALU.mult, ALU.add)
                                        yield
                                        act(AF.Square, SQ, SQ[:], YC, YC[:])
                                        yield
                                        bk = bank()
                                        mm(bk, bk[:], CBt, BD, SQ, SQ[:], sig=True)
                                        yield
                                        act(AF.Ln, RS, RS[:], bk, bk[:], scale=1.0 / 64.0, bias=EPS[:, 1:2], rd=[EPS])
                                        yield
                                        act(AF.Exp, RS, RS[:], RS, RS[:], scale=-0.5)
                                        yield
                                        tt(YC, YC[:], YC, YC[:], RS, RS[:], ALU.mult)
                                        yield
                                        ts(YC, YC[:], YC, YC[:], col(C_GNW + m), ALU.mult, col(C_GNB + m), ALU.add, rd=[CC])
                                        yield
                                        tt(YC, YC[:], YC, YC[:], BON[m], BON[m][:], ALU.add)
                                        yield
                                        tt(OTB, OTB[:, 4 + m, :], YC, YC[:], GG[m], GG[m][:], ALU.mult)

                                    drain(gn(0), gn(1), gn(2), gn(3))

                    except _Skip:
                        memset(OTB, OTB[:, 4:8, :], 0.0)
                        for _u in range(4):
                            next_unit()
                            prefetch()
                    for j in range(8):
                        if j % 4 == 0:
                            wb = next_unit()
                            prefetch()
                        bk = bank()
                        for k in range(8):
                            mm(bk, bk[:], wb[k], wb[k][:, (j % 4) * 128:(j % 4 + 1) * 128], OTB, OTB[:, k, :], start=(k == 0), stop=(k == 7), sig=(k == 7))
                        tt(H[j], H[j][:], bk, bk[:], H[j], H[j][:], ALU.add)

                chk(2)
                chk(3)
                with Sub(get_tail()) as ph:
                    XNK = ksplit(ph.sb("XN2", [128, 8, NB], BF16))
                    HID = [ph.sb("hid%d" % i, [128, 8, NB], BF16) for i in range(2)]
                    RL = [ph.sb("rl%d" % i, [128, NB], F32) for i in range(2)]
                    with Sub(get_tail()) as p1:
                        rmsnorm(p1, C_LNMLP, XNK, None)
                    if b + 1 < nblk and INTERLEAVE_PROLOGUE:
                        pro[0] = prologue(b + 1)

                    def step():
                        if pro[0] is not None:
                            try:
                                next(pro[0])
                            except StopIteration:
                                pro[0] = None

                    for g in range(4):
                        hid = HID[g % 2]
                        bks8 = dense8_kouter(XNK) if g == 0 else None
                        for j in range(8):
                            if bks8 is not None:
                                bk = bks8[j]
                            else:
                                if j % 4 == 0:
                                    wb = next_unit()
                                    prefetch()
                                bk = bank()
                                for k in range(8):
                                    mm(bk, bk[:], wb[k], wb[k][:, (j % 4) * 128:(j % 4 + 1) * 128], XNK[k], XNK[k][:, :], start=(k == 0), stop=(k == 7), sig=(k == 7))
                            rl = RL[j % 2]
                            act(AF.Relu, rl, rl[:], bk, bk[:])
                            tt(hid, hid[:, j, :], rl, rl[:], rl, rl[:], ALU.mult)
                            step()
                        for j in range(8):
                            if j % 4 == 0:
                                wb = next_unit()
                                prefetch()
                            bk = bank()
                            for k in range(8):
                                mm(bk, bk[:], wb[k], wb[k][:, (j % 4) * 128:(j % 4 + 1) * 128], hid, hid[:, k, :], start=(k == 0), stop=(k == 7), sig=(k == 7))
                            tt(H[j], H[j][:], bk, bk[:], H[j], H[j][:], ALU.add)
                            step()

                chk(4)
                with Sub(get_tail()) as ph:
                    XNK = ksplit(ph.sb("XN3", [128, 8, NB], BF16))
                    G = [ph.sb("pg%d" % j, [128, NB], F32) for j in range(8)]
                    TM = [ph.sb("ptm%d" % j, [128, NB], F32) for j in range(2)]
                    with Sub(get_tail()) as p1:
                        rmsnorm(p1, C_LNPLE, XNK, None)
                    bks8 = dense8_kouter(XNK)
                    for j in range(8):
                        bk = bks8[j]
                        act(AF.Sigmoid, G[j], G[j][:], bk, bk[:])
                    for j in range(8):
                        if j % 4 == 0:
                            wb = next_unit()
                            prefetch()
                        bk = bank()
                        for k in range(2):
                            mm(bk, bk[:], wb[k], wb[k][:, (j % 4) * 128:(j % 4 + 1) * 128], PT, PT[:, k, :], start=(k == 0), stop=(k == 1), sig=(k == 1))
                        tm = TM[j % 2]
                        tt(tm, tm[:], bk, bk[:], G[j], G[j][:], ALU.mult)
                        tt(H[j], H[j][:], tm, tm[:], H[j], H[j][:], ALU.add, e="dve")

            except _Skip:
                pass
            with Sub(get_tail()) as ph:
                XF = ph.sb("XF", [128, 8, NB], F32)
                with Sub(get_tail()) as p1:
                    rmsnorm(p1, C_LNF, XF, lambda k: XF[:, k, :])
                for tq in range(4):
                    xs = XS[tq % 2]
                    b0, b1 = bank(), bank()
                    for k in range(8):
                        ob = b0 if k < 4 else b1
                        tr(ob, ob[:, (k % 4) * 128:(k % 4 + 1) * 128], XF, XF[:, k, tq * 128:(tq + 1) * 128], CF, IDF, sig=(k % 4 == 3))
                    cp("act", xs, xs[:, 0:512], b0, b0[:])
                    cp("dve", xs, xs[:, 512:1024], b1, b1[:])
                    r0 = (tq // 2) * (TOK // 2) + b * (NB // 2) + (tq % 2) * 128
                    out_dmas.append(S.dma("sp", out[r0:r0 + 128, :], xs[:], reads=[xs]))
            tailp[0].__exit__(None, None, None)
            tailp[0] = None

        S.wait_all_dma("sp")
        S.barrier(("pe", "act", "dve", "pool"))
    return nc


_NC_CACHE = {}


def kernel(**inputs):
    i = {k: np.asarray(v) for k, v in inputs.items()}
    x = np.ascontiguousarray(i["x"], dtype=np.float32)
    p = np.ascontiguousarray(i["p"], dtype=np.float32)[0]
    cb, cf = _const_arrays()
    shared = {
        "w_in": np.ascontiguousarray(i["w_in"][0], dtype=np.float32),
        "w_out": np.ascontiguousarray(i["w_out"][0], dtype=np.float32),
        "w_ff1": np.ascontiguousarray(i["w_ff1"][0], dtype=np.float32),
        "w_ff2": np.ascontiguousarray(i["w_ff2"][0], dtype=np.float32),
        "w_pg": np.ascontiguousarray(i["w_ple_gate"][0], dtype=np.float32),
        "w_pp": np.ascontiguousarray(i["w_ple_proj"][0], dtype=np.float32),
        "gk_w": np.ascontiguousarray(i["gla_gk_w"][0], dtype=np.float32),
        "w2": np.ascontiguousarray(i["rwkv_w2"][0], dtype=np.float32),
        "a2": np.ascontiguousarray(i["rwkv_a2"][0], dtype=np.float32),
        "g2": np.ascontiguousarray(i["rwkv_g2"][0], dtype=np.float32),
        "vecs": _pack_vecs(i),
        "cb": cb,
        "cf": cf,
    }
    in_maps = []
    for c in range(NCORES):
        m = dict(shared)
        m["x"] = x[2 * c:2 * c + 2].reshape(TOK, 1024)
        m["p"] = p[2 * c:2 * c + 2].reshape(TOK, 256)
        in_maps.append(m)
    if "nc" not in _NC_CACHE:
        _NC_CACHE["nc"] = build_nc()
    res = run_bass_kernel_spmd(_NC_CACHE["nc"], in_maps, core_ids=list(range(NCORES)))
    outs = [np.asarray(r["out"], dtype=np.float32).reshape(2, 2048, 1024) for r in res.results]
    return np.concatenate(outs, axis=0)
```

```python
import numpy as np
import concourse.bass as bass
import concourse.mybir as mybir
from concourse.bass_utils import run_bass_kernel_spmd
from contextlib import ExitStack

F32 = mybir.dt.float32
BF16 = mybir.dt.bfloat16
AF = mybir.ActivationFunctionType
ALU = mybir.AluOpType

NCORES = 8
TOK = 4096
NB = 512
NBLK = TOK // NB
CDEC = float(np.exp(-0.5))
INTERLEAVE_PROLOGUE = False
POOL_E = "dve"


class T:
    __slots__ = ("t", "w", "r", "name")

    def __init__(self, t, name=""):
        self.t = t
        self.w = None
        self.r = {}
        self.name = name

    def __getitem__(self, k):
        return self.t[k]


class Sched:
    NDMA = 56

    def __init__(self, nc, es):
        self.nc = nc
        self.eng = {"pe": nc.tensor, "act": nc.scalar, "dve": nc.vector,
                    "pool": nc.gpsimd, "sp": nc.sync}
        self.sem = {k: es.enter_context(nc.semaphore("s_" + k)) for k in self.eng}
        self.cnt = {k: 0 for k in self.eng}
        self.pend = {k: False for k in self.eng}
        self.seen = {k: {} for k in self.eng}
        self.dsem = [es.enter_context(nc.semaphore("d%d" % i)) for i in range(self.NDMA)]
        self.dcnt = [0] * self.NDMA
        self.dnext = 0
        self.ninst = 0
        self.frontier = {}

    def _semof(self, key):
        return self.dsem[key] if isinstance(key, int) else self.sem[key]

    def _wait(self, e, key, idx):
        if idx is None or idx <= 0:
            return
        if self.seen[e].get(key, 0) >= idx:
            return
        if not isinstance(key, int):
            assert idx <= self.cnt[key], ("wait on unsignaled", e, key, idx, self.cnt[key])
        self.eng[e].wait_ge(self._semof(key), idx)
        self.seen[e][key] = idx

    def _deps(self, e, reads, writes, is_dma=False):
        for t in reads:
            if t.w is not None:
                self._wait(e, *t.w)
        for t in writes:
            if t.w is not None:
                if not (e == "pe" and t.w[0] == "pe"):
                    self._wait(e, *t.w)
            for k, idx in t.r.items():
                if k == e == "pe":
                    continue
                self._wait(e, k, idx)

    def op(self, e, fn, reads=(), writes=(), signal=True):
        self._deps(e, reads, writes)
        ins = fn()
        self.ninst += 1
        if signal:
            ins.then_inc(self.sem[e], 1)
            self.cnt[e] += 1
            idx = self.cnt[e]
            self.pend[e] = False
        else:
            idx = self.cnt[e] + 1
            self.pend[e] = True
        for t in writes:
            t.w = (e, idx)
            t.r = {}
        for t in reads:
            if t.r.get(e, 0) < idx:
                t.r[e] = idx
        return ins

    def dma(self, q, out_ap, in_ap, reads=(), writes=()):
        s = self.dnext
        self.dnext = (self.dnext + 1) % self.NDMA
        self._wait(q, s, self.dcnt[s])
        self._deps(q, reads, writes, True)
        ins = self.eng[q].dma_start(out=out_ap, in_=in_ap)
        ins.then_inc(self.dsem[s], 16)
        self.dcnt[s] += 16
        idx = self.dcnt[s]
        for t in writes:
            t.w = (s, idx)
            t.r = {}
        for t in reads:
            t.r[s] = idx
        return s, idx

    def dma_fresh(self, es, q, out_ap, in_ap, reads=(), writes=()):
        s = len(self.dsem)
        self.dsem.append(es.enter_context(self.nc.semaphore("f%d" % s)))
        self.dcnt.append(0)
        self._deps(q, reads, writes, True)
        ins = self.eng[q].dma_start(out=out_ap, in_=in_ap)
        ins.then_inc(self.dsem[s], 16)
        self.dcnt[s] = 16
        for t in writes:
            t.w = (s, 16)
            t.r = {}
        for t in reads:
            t.r[s] = 16
        return s, 16

    def release_point(self, engs=("pe", "act", "dve", "pool")):
        for e in engs:
            assert not self.pend[e], ("release with unsignaled tail", e)
            self.frontier[e] = self.cnt[e]

    def barrier(self, engs=("pe", "act", "dve")):
        for e in engs:
            assert not self.pend[e], ("barrier with unsignaled tail", e)
        for e in engs:
            for k in engs:
                if k != e:
                    self._wait(e, k, self.cnt[k])

    def wait_all_dma(self, q="sp"):
        for s in range(len(self.dsem)):
            self._wait(q, s, self.dcnt[s])


class _Skip(Exception):
    pass


class Phase:
    def __init__(self, S, nc):
        self.S, self.nc = S, nc
        self.es = ExitStack()
        self.n = 0

    def __enter__(self):
        self.es.__enter__()
        return self

    CNT = [0]

    def sb(self, name, shape, dt):
        Phase.CNT[0] += 1
        t = T(self.es.enter_context(self.nc.sbuf_tensor("%s_%d" % (name, Phase.CNT[0]), shape, dt)), name)
        t.r = dict(self.S.frontier)
        return t

    def __exit__(self, *a):
        try:
            if a[0] is None or a[0] is _Skip:
                self.S.release_point()
        finally:
            if a[0] is _Skip:
                self.es.__exit__(None, None, None)
                r = False
            else:
                r = self.es.__exit__(*a)
        return r


def _const_arrays():
    p = np.arange(128)
    pl = p % 64
    ident = np.eye(128, dtype=np.float32)
    ones = np.ones((128, 128), np.float32)
    bd = (p[:, None] // 64 == p[None, :] // 64).astype(np.float32)
    i64 = np.arange(64)
    maskg = np.tile((pl[:, None] <= i64[None, :]).astype(np.float32), (1, 4))
    strict = (pl[:, None] < i64[None, :]).astype(np.float32)
    incl = (pl[:, None] <= i64[None, :]).astype(np.float32)
    mrw = np.tile(np.concatenate([strict, incl], axis=1), (1, 4))
    mq = np.tile((i64[None, :] < pl[:, None]).astype(np.float32), (1, 8))
    i8 = np.tile((i64[None, :] == pl[:, None]).astype(np.float32), (1, 8))
    cb = np.concatenate([ident, ones, bd, maskg, mrw, mq, i8], axis=1)
    mscan = np.ones((128, 512), np.float32)
    mscan[:, 0::64] = 0.0
    cf = np.concatenate([ident, mscan], axis=1)
    return np.ascontiguousarray(cb), np.ascontiguousarray(cf)


CB_ID, CB_ONES, CB_BD, CB_MG, CB_MRW, CB_MQ, CB_I8 = 0, 128, 256, 384, 640, 1152, 1664
CB_W = 2176

C_LNMIX, C_LNMLP, C_LNPLE, C_LNF, C_CONV, C_GKB, C_NG, C_MU = 0, 8, 16, 24, 32, 64, 66, 67
C_W0, C_A0, C_KK, C_KA, C_RK, C_GNW, C_GNB, C_MULR, C_MUG = 79, 83, 87, 91, 95, 99, 103, 107, 108
C_NGKB, C_1MKA = 109, 111


def _pack_vecs(i):
    rows = np.zeros((128, 128), np.float32)

    def put(r0, v):
        v = np.asarray(v, np.float32).reshape(-1)
        n = (v.size + 127) // 128
        buf = np.zeros(n * 128, np.float32)
        buf[:v.size] = v
        rows[r0:r0 + n] = buf.reshape(n, 128)

    put(C_LNMIX, i["ln_mix"]); put(C_LNMLP, i["ln_mlp"]); put(C_LNPLE, i["ln_ple"]); put(C_LNF, i["ln_final"])
    put(C_CONV, i["gla_conv_w"]); put(C_GKB, i["gla_gk_b"]); put(C_NG, i["gla_norm_g"])
    mu = np.asarray(i["rwkv_mu"], np.float32).reshape(-1)
    put(C_MU, mu[:1536]); put(C_MULR, mu[1536:1600]); put(C_MUG, mu[1600:1696])
    put(C_W0, i["rwkv_w0"]); put(C_A0, i["rwkv_a0"]); put(C_KK, i["rwkv_k_k"]); put(C_KA, i["rwkv_k_a"])
    put(C_RK, i["rwkv_r_k"]); put(C_GNW, i["rwkv_gn_w"]); put(C_GNB, i["rwkv_gn_b"])
    return np.ascontiguousarray(rows.T)


def build_nc():
    nc = bass.Bass("TRN2", target_bir_lowering=False)
    import os
    nblk = int(os.environ.get("KNBLK", NBLK))
    stage = int(os.environ.get("KSTAGE", 99))

    ksub = int(os.environ.get("KSUB", 99))

    def sub(n):
        if ksub == n:
            raise _Skip()

    def chk(n):
        if stage == n:
            raise _Skip()

    def din(name, shape):
        return nc.dram_tensor(name, shape, F32, kind="ExternalInput").ap()

    x = din("x", [TOK, 1024])
    pin = din("p", [TOK, 256])
    w_in = din("w_in", [1024, 3248])
    w_out = din("w_out", [1024, 1024])
    w_ff1 = din("w_ff1", [1024, 4096])
    w_ff2 = din("w_ff2", [4096, 1024])
    w_pg = din("w_pg", [1024, 1024])
    w_pp = din("w_pp", [256, 1024])
    gkw_d = din("gk_w", [16, 256])
    w2_d = din("w2", [32, 512])
    a2_d = din("a2", [32, 512])
    g2_d = din("g2", [96, 512])
    vecs_d = din("vecs", [128, 128])
    cb_d = din("cb", [128, CB_W])
    cf_d = din("cf", [128, 640])
    out = nc.dram_tensor("out", [TOK, 1024], F32, kind="ExternalOutput").ap()

    with ExitStack() as es:
        S = Sched(nc, es)

        def sb(name, shape, dt):
            return T(es.enter_context(nc.sbuf_tensor(name, shape, dt)), name)

        banks = [T(es.enter_context(nc.psum_tensor("ps%d" % i, [128, 512], F32)), "ps%d" % i) for i in range(8)]
        bstate = [0]
        bank_hist = {}

        def bank():
            b = banks[bstate[0] % 8]
            bstate[0] += 1
            bank_hist.pop(id(b), None)
            return b

        def bf(b):
            return b.t[:].bitcast(BF16)

        CC = sb("CC", [128, 128], F32)
        CBt = sb("CB", [128, CB_W], BF16)
        CF = sb("CF", [128, 640], F32)
        GKW = sb("GKW", [16, 256], BF16)
        W2T = sb("W2T", [32, 512], BF16)
        A2T = sb("A2T", [64, 512], BF16)
        G2T = sb("G2T", [96, 512], BF16)
        NWB = 4
        WBF = [sb("WB%d" % i, [128, 8, 512], BF16) for i in range(NWB)]
        WB = [[T(WBF[i][:, k, :], "WB%d_%d" % (i, k)) for k in range(8)] for i in range(NWB)]
        HT = [sb("HT0", [128, 8, NB], F32)] * 2
        Hs = [[T(HT[0][:, k, :], "H0_%d" % k) for k in range(8)]] * 2
        PTs = [sb("PT0", [128, 2, NB], BF16)] * 2
        XN1 = sb("XN1", [128, 8, NB], BF16)
        XN1K = [T(XN1[:, k, :], "XN1_k%d" % k) for k in range(8)]
        hcur = [None]
        XS = [sb("XS%d" % i, [128, 1024], F32) for i in range(2)]
        PSt = [sb("PSt%d" % i, [128, 256], F32) for i in range(2)]
        GC = sb("GC", [128, 8, 2, 4], BF16)
        RC = sb("RC", [128, 14, 2, 4], F32)
        GST = [sb("GST%d" % i, [128, 2, 128], F32) for i in range(2)]
        GSTB = [sb("GSTB%d" % i, [128, 2, 128], BF16) for i in range(2)]
        TS = [sb("TS%d" % i, [128, 4, 64], F32) for i in range(2)]
        TSB = [[sb("TSB%d_%d" % (i, q), [128, 4, 64], BF16) for q in range(2)] for i in range(2)]

        IDB = CBt[:, CB_ID:CB_ID + 128]
        ONES = CBt[:, CB_ONES:CB_ONES + 128]
        BD = CBt[:, CB_BD:CB_BD + 128]
        IDF = CF[:, 0:128]
        MSCAN = CF[:, 128:640]

        def mm(oT, o_ap, lT, l_ap, rT, r_ap, start=True, stop=True, sig=False, force=True):
            small = l_ap.partition_size() <= 64
            if small:
                sig = sig or force
                rg = l_ap.base_partition()
                lo = o_ap.base_partition()
                hi = lo + o_ap.partition_size()
                hist = bank_hist.setdefault(id(oT), [])
                for (rg2, lo2, hi2, idx2) in hist:
                    if rg2 != rg and lo < hi2 and lo2 < hi:
                        S._wait("pe", "pe", idx2)
                hist[:] = [e for e in hist if S.seen["pe"].get("pe", 0) < e[3]]
            else:
                bank_hist.pop(id(oT), None)
            S.op("pe", lambda: nc.tensor.matmul(o_ap, lhsT=l_ap, rhs=r_ap, start=start, stop=stop),
                 reads=[lT, rT], writes=[oT], signal=sig)
            if small:
                hist.append((rg, lo, hi, S.cnt["pe"] if sig else S.cnt["pe"] + 1))

        def tr(oT, o_ap, iT, i_ap, idT, id_ap, sig=False):
            S.op("pe", lambda: nc.tensor.transpose(out=o_ap, in_=i_ap, identity=id_ap),
                 reads=[iT, idT], writes=[oT], signal=sig)

        def act(func, oT, o_ap, iT, i_ap, scale=1.0, bias=None, rd=()):
            def f():
                if bias is None:
                    return nc.scalar.activation(out=o_ap, in_=i_ap, func=func, scale=scale)
                return nc.scalar.activation(out=o_ap, in_=i_ap, func=func, scale=scale, bias=bias)
            S.op("act", f, reads=[iT, *rd], writes=[oT])

        def tt(oT, o_ap, aT, a_ap, bT, b_ap, op, e="dve"):
            S.op(e, lambda: S.eng[e].tensor_tensor(out=o_ap, in0=a_ap, in1=b_ap, op=op),
                 reads=[aT, bT], writes=[oT])

        def ts(oT, o_ap, aT, a_ap, s1, op0, s2=None, op1=None, rd=(), e="dve"):
            def f():
                if op1 is None:
                    return S.eng[e].tensor_scalar(out=o_ap, in0=a_ap, scalar1=s1, scalar2=None, op0=op0)
                return S.eng[e].tensor_scalar(out=o_ap, in0=a_ap, scalar1=s1, scalar2=s2, op0=op0, op1=op1)
            S.op(e, f, reads=[aT, *rd], writes=[oT])

        def stt(oT, o_ap, aT, a_ap, sc, bT, b_ap, op0, op1, rd=()):
            S.op("dve", lambda: nc.vector.scalar_tensor_tensor(out=o_ap, in0=a_ap, scalar=sc, in1=b_ap, op0=op0, op1=op1),
                 reads=[aT, bT, *rd], writes=[oT])

        def cp(e, oT, o_ap, iT, i_ap):
            if e == "act":
                S.op("act", lambda: nc.scalar.copy(out=o_ap, in_=i_ap), reads=[iT], writes=[oT])
            else:
                S.op("dve", lambda: nc.vector.tensor_copy(out=o_ap, in_=i_ap), reads=[iT], writes=[oT])

        def recip(oT, o_ap, iT, i_ap):
            S.op("dve", lambda: nc.vector.reciprocal(out=o_ap, in_=i_ap), reads=[iT], writes=[oT])

        def memset(oT, o_ap, v):
            S.op("dve", lambda: nc.vector.memset(o_ap, v), writes=[oT])

        def col(c):
            return CC[:, c:c + 1]

        S.dma("sp", CC[:], vecs_d, writes=[CC])
        S.dma("sp", CF[:], cf_d, writes=[CF])
        S.dma_fresh(es, "pool", CBt[:], cb_d, writes=[CBt])
        S.dma_fresh(es, "pool", GKW[:], gkw_d, writes=[GKW])
        S.dma_fresh(es, "pool", W2T[:], w2_d, writes=[W2T])
        S.dma_fresh(es, "pool", A2T[32:64, :], a2_d, writes=[A2T])
        S.dma_fresh(es, "pool", G2T[:], g2_d, writes=[G2T])
        ts(CC, CC[:, C_NGKB:C_NGKB + 2], CC, CC[:, C_GKB:C_GKB + 2], -1.0, ALU.mult)
        ts(CC, CC[:, C_1MKA:C_1MKA + 4], CC, CC[:, C_KA:C_KA + 4], -1.0, ALU.mult, 1.0, ALU.add)

        def mk_scratch(name, w_ap, rows, cols, by_rows=False):
            sc = nc.dram_tensor(name, [rows, cols], BF16, kind="Internal").ap()
            chunks = []
            if by_rows:
                for r in range(0, rows, 1024):
                    chunks.append((r, min(r + 1024, rows), 0, cols))
            else:
                for c in range(0, cols, 1024):
                    chunks.append((0, rows, c, min(c + 1024, cols)))
            return {"sc": sc, "src": w_ap, "chunks": [(ch, T(None, name)) for ch in chunks]}

        WS = {
            "in": mk_scratch("wb_in", w_in, 1024, 3248),
            "out": mk_scratch("wb_out", w_out, 1024, 1024),
            "ff1": mk_scratch("wb_ff1", w_ff1, 1024, 4096),
            "ff2": mk_scratch("wb_ff2", w_ff2, 4096, 1024, by_rows=True),
            "pg": mk_scratch("wb_pg", w_pg, 1024, 1024),
            "pp": mk_scratch("wb_pp", w_pp, 256, 1024),
        }

        def convert(key, idxs=None):
            w = WS[key]
            for ci, ((r0_, r1_, c0_, c1_), tch) in enumerate(w["chunks"]):
                if idxs is not None and ci not in idxs:
                    continue
                S.dma_fresh(es, "pool", w["sc"][r0_:r1_, c0_:c1_], w["src"][r0_:r1_, c0_:c1_], writes=[tch])

        convert("in")
        convert("out")
        for g_ in range(4):
            convert("ff1", [g_])
            convert("ff2", [g_])
        convert("pg")
        convert("pp")

        def unit(key, r0, kc, c0, ncol):
            w = WS[key]
            deps = [tch for ((a0_, a1_, b0_, b1_), tch) in w["chunks"]
                    if a0_ < r0 + kc * 128 and r0 < a1_ and b0_ < c0 + ncol and c0 < b1_]
            return (w["sc"], r0, kc, c0, ncol, deps)

        units = []
        for b in range(NBLK):
            units.append(unit("in", 0, 8, 0, 512))
            units.append(unit("in", 0, 8, 512, 512))
            units.append(unit("in", 0, 8, 1024, 512))
            units.append(unit("in", 0, 8, 1536, 16))
            units.append(unit("in", 0, 8, 1552, 512))
            units.append(unit("in", 0, 8, 2064, 512))
            units.append(unit("in", 0, 8, 2576, 512))
            units.append(unit("in", 0, 8, 3088, 160))
            for hf in range(2):
                units.append(unit("out", 0, 8, hf * 512, 512))
            for g in range(4):
                for hf in range(2):
                    units.append(unit("ff1", 0, 8, g * 1024 + hf * 512, 512))
                for hf in range(2):
                    units.append(unit("ff2", g * 1024, 8, hf * 512, 512))
            for hf in range(2):
                units.append(unit("pg", 0, 8, hf * 512, 512))
            for hf in range(2):
                units.append(unit("pp", 0, 2, hf * 512, 512))
        ustate = {"issued": 0, "used": 0}

        def issue_unit():
            i = ustate["issued"]
            if i >= len(units):
                return
            w_ap, r0, kc, c0, ncol, deps = units[i]
            wb = WB[i % NWB]
            S.dma("sp", WBF[i % NWB][:, 0:kc, 0:ncol], w_ap[r0:r0 + kc * 128, c0:c0 + ncol].rearrange("(k p) c -> p k c", p=128),
                  reads=deps, writes=wb[0:kc])
            ustate["issued"] += 1

        def next_unit():
            i = ustate["used"]
            while ustate["issued"] <= i:
                issue_unit()
            ustate["used"] += 1
            return WB[i % NWB]

        def prefetch():
            while ustate["issued"] < min(ustate["used"] + NWB - 1, len(units)):
                issue_unit()

        issue_unit()

        def ksplit(t):
            out_ = []
            for k in range(8):
                tk = T(t.t[:, k, :], t.name + "_k%d" % k)
                tk.w, tk.r = t.w, dict(t.r)
                out_.append(tk)
            return out_

        def dense8_kouter(rhsK):
            wb0 = next_unit()
            prefetch()
            wb1 = next_unit()
            bks = [bank() for _ in range(8)]
            for k in range(8):
                for j in range(8):
                    wb = wb0 if j < 4 else wb1
                    mm(bks[j], bks[j][:], wb[k], wb[k][:, (j % 4) * 128:(j % 4 + 1) * 128], rhsK[k], rhsK[k][:, :],
                       start=(k == 0), stop=(k == 7), sig=(k == 7))
            prefetch()
            return bks

        def rmsnorm_gen(Hl, SQ, RS, gcol, dst, dst_ap_fn, n_feat=1024.0):
            bk = bank()
            for k in range(8):
                sq = SQ[k % 2]
                act(AF.Square, sq, sq[:], Hl[k], Hl[k][:])
                mm(bk, bk[:], CBt, ONES, sq, sq[:], start=(k == 0), stop=(k == 7), sig=True)
                if k % 2 == 1:
                    yield
            act(AF.Ln, RS, RS[:], bk, bk[:], scale=1.0 / n_feat, bias=EPS[:, 0:1], rd=[EPS])
            act(AF.Exp, RS, RS[:], RS, RS[:], scale=-0.5)
            yield
            for k in range(8):
                if isinstance(dst, list):
                    stt(dst[k], dst[k][:, :], Hl[k], Hl[k][:], col(gcol + k), RS, RS[:], ALU.mult, ALU.mult, rd=[CC])
                else:
                    stt(dst, dst_ap_fn(k), Hl[k], Hl[k][:], col(gcol + k), RS, RS[:], ALU.mult, ALU.mult, rd=[CC])
                if k % 2 == 1:
                    yield

        def rmsnorm(ph, gcol, dst, dst_ap_fn, n_feat=1024.0):
            SQ = [ph.sb("nsq%d" % i, [128, NB], BF16) for i in range(2)]
            RS = ph.sb("nrs", [128, NB], F32)
            for _ in rmsnorm_gen(hcur[0], SQ, RS, gcol, dst, dst_ap_fn, n_feat):
                pass

        def prologue(b):
            Hl = Hs[b % 2]
            ht = HT[b % 2]
            pt = PTs[b % 2]
            for tq in range(4):
                xs = XS[tq % 2]
                r0 = (tq // 2) * (TOK // 2) + b * (NB // 2) + (tq % 2) * 128
                S.dma("sp", xs[:], x[r0:r0 + 128, :], writes=[xs])
                for half in range(2):
                    bk = bank()
                    for kk in range(4):
                        k = half * 4 + kk
                        tr(bk, bk[:, kk * 128:(kk + 1) * 128], xs, xs[:, k * 128:(k + 1) * 128], CF, IDF, sig=(kk == 3))
                    e = "act" if half else "dve"
                    o_ap = ht[:, half * 4:half * 4 + 4, tq * 128:(tq + 1) * 128]
                    i_ap = bk[:, :].rearrange("p (k t) -> p k t", t=128)
                    if e == "act":
                        S.op("act", lambda: nc.scalar.copy(out=o_ap, in_=i_ap), reads=[bk], writes=Hl[half * 4:half * 4 + 4])
                    else:
                        S.op("dve", lambda: nc.vector.tensor_copy(out=o_ap, in_=i_ap), reads=[bk], writes=Hl[half * 4:half * 4 + 4])
                    yield
            for tq in range(4):
                pst = PSt[tq % 2]
                r0 = (tq // 2) * (TOK // 2) + b * (NB // 2) + (tq % 2) * 128
                S.dma("sp", pst[:], pin[r0:r0 + 128, :], writes=[pst])
                bk = bank()
                for c2 in range(2):
                    tr(bk, bk[:, c2 * 128:(c2 + 1) * 128], pst, pst[:, c2 * 128:(c2 + 1) * 128], CF, IDF, sig=(c2 == 1))
                cp("act", pt, pt[:, :, tq * 128:(tq + 1) * 128], bk, bk[:, 0:256].rearrange("p (c t) -> p c t", t=128))
                yield
            with Phase(S, nc) as pp:
                NSQ = [pp.sb("NSQ%d" % i, [128, NB], BF16) for i in range(2)]
                NRS = pp.sb("NRS", [128, NB], F32)
                for _ in rmsnorm_gen(Hl, NSQ, NRS, C_LNMIX, XN1K, None):
                    yield

        EPS = sb("EPS", [128, 4], F32)
        memset(EPS, EPS[:, 0:1], 1e-6)
        memset(EPS, EPS[:, 1:2], 64e-5)
        memset(EPS, EPS[:, 2:3], 1.0)
        memset(EPS, EPS[:, 3:4], 1e-19)

        out_dmas = []

        class Sub:
            def __init__(self, p):
                self.p = p

            def __enter__(self):
                return self

            def sb(self, *a):
                return self.p.sb(*a)

            def __exit__(self, *a):
                return False

        tailp = [None]

        def get_tail():
            if tailp[0] is None:
                tailp[0] = Phase(S, nc)
                tailp[0].__enter__()
            return tailp[0]

        pro = [prologue(0)]
        for b in range(nblk):
            t0 = b * NB
            if b == 0:
                memset(GC, GC[:], 0.0)
                memset(RC, RC[:], 0.0)
                for i_ in range(2):
                    memset(GST[i_], GST[i_][:], 0.0)
                    memset(GSTB[i_], GSTB[i_][:], 0.0)
                    memset(TS[i_], TS[i_][:], 0.0)
                    memset(TSB[i_][0], TSB[i_][0][:], 0.0)
                    memset(TSB[i_][1], TSB[i_][1][:], 0.0)
            try:
                H = Hs[b % 2]
                hcur[0] = H
                PT = PTs[b % 2]
                XNK = XN1K
                if pro[0] is None:
                    pro[0] = prologue(b)
                for _ in pro[0]:
                    pass
                pro[0] = None

                with Phase(S, nc) as mx:
                    OTB = mx.sb("OTB", [128, 8, NB], BF16)

                    chk(1)
                    with Phase(S, nc) as ph:
                        ZC = [ph.sb("zc%d" % j, [128, 2, 4 + NB // 2], BF16) for j in range(8)]
                        SG = [ph.sb("sg%d" % j, [128, NB], F32) for j in range(4)]
                        GL = ph.sb("gl", [16, NB], BF16)
                        DG = ph.sb("dg", [128, 32, 128], BF16)
                        for kk in range(4):
                            for j in range(8):
                                ts(DG, DG[:, kk * 8 + j, :], CBt, IDB, col(C_CONV + kk * 8 + j), ALU.mult, rd=[CC])
                        QS = [ph.sb("qs%d" % j, [128, NB], F32) for j in range(2)]
                        KS = [ph.sb("ks%d" % j, [128, NB], F32) for j in range(2)]
                        QT = [ph.sb("qt%d" % j, [128, NB], BF16) for j in range(2)]
                        KT = [ph.sb("kt%d" % j, [128, NB], BF16) for j in range(2)]
                        VBF = [ph.sb("vbf%d" % j, [128, NB], BF16) for j in range(4)]
                        EB = [ph.sb("eb%d" % j, [128, NB], F32) for j in range(2)]
                        ENB = [ph.sb("enb%d" % j, [128, NB], F32) for j in range(2)]
                        E1 = ph.sb("e1", [128, NB], F32)
                        KVT = [ph.sb("kvt%d" % j, [128, 768], BF16) for j in range(4)]
                        AT = [ph.sb("at%d" % j, [128, 256], BF16) for j in range(4)]
                        OTA = ph.sb("ota", [128, 4, NB], F32)
                        TMP = [ph.sb("tmpg%d" % i_, [128, 2, 128], F32) for i_ in range(2)]
                        SQ = ph.sb("sqg", [128, NB], BF16)
                        RS = ph.sb("rsg", [128, NB], F32)
                        T1 = ph.sb("t1g", [128, NB], F32)

                        bks8 = dense8_kouter(XNK)
                        for j in range(8):
                            bk = bks8[j]
                            cp("dve", ZC[j], ZC[j][:, :, 0:4], GC, GC[:, j, :, :])
                            cp("act", ZC[j], ZC[j][:, :, 4:4 + NB // 2], bk, bk[:, :].rearrange("p (s t) -> p s t", s=2))
                            cp("dve", GC, GC[:, j, :, :], ZC[j], ZC[j][:, :, NB // 2:NB // 2 + 4])
                        wb = next_unit()
                        prefetch()
                        for j in range(4):
                            bk = bank()
                            for k in range(8):
                                mm(bk, bk[:], wb[k], wb[k][:, (j % 4) * 128:(j % 4 + 1) * 128], XNK[k], XNK[k][:, :], start=(k == 0), stop=(k == 7), sig=(k == 7))
                            act(AF.Silu, SG[j], SG[j][:], bk, bk[:])
                        wb = next_unit()
                        prefetch()
                        bk = bank()
                        for k in range(8):
                            mm(bk, bk[0:16, :], wb[k], wb[k][:, 0:16], XNK[k], XNK[k][:, :], start=(k == 0), stop=(k == 7), sig=(k == 7))
                        cp("act", GL, GL[:], bk, bk[0:16, :])
                        sub(1)
                        for j in range(8):
                            z = ZC[j]
                            bk = bank()
                            for kk in range(4):
                                mm(bk, bk[:, :].rearrange("p (s t) -> p s t", s=2), DG, DG[:, kk * 8 + j, :], z, z[:, :, 1 + kk:1 + kk + NB // 2],
                                   start=(kk == 0), stop=(kk == 3), sig=(kk == 3))
                            if j < 2:
                                act(AF.Silu, QS[j], QS[j][:], bk, bk[:])
                            elif j < 4:
                                act(AF.Silu, KS[j - 2], KS[j - 2][:], bk, bk[:])
                            else:
                                act(AF.Silu, VBF[j - 4], VBF[j - 4][:], bk, bk[:])
                        sub(2)
                        for m in range(2):
                            bk = bank()
                            mm(bk, bk[:], GKW, GKW[0:16, m * 128:(m + 1) * 128], GL, GL[0:16, :], sig=True)
                            act(AF.Exp, E1, E1[:], bk, bk[:], scale=-1.0, bias=col(C_NGKB + m), rd=[CC])
                            act(AF.Ln, E1, E1[:], E1, E1[:], bias=EPS[:, 2:3], rd=[EPS])
                            S.op("dve", lambda: nc.vector.tensor_tensor_scan(out=EB[m][:], data0=MSCAN, data1=E1[:], initial=0.0, op0=ALU.mult, op1=ALU.add),
                                 reads=[CF, E1], writes=[EB[m]])
                            act(AF.Exp, ENB[m], ENB[m][:], EB[m], EB[m][:], scale=1.0 / 16.0)
                            act(AF.Exp, EB[m], EB[m][:], EB[m], EB[m][:], scale=-1.0 / 16.0)
                            stt(QT[m], QT[m][:], QS[m], QS[m][:], 0.125, EB[m], EB[m][:], ALU.mult, ALU.mult)
                            tt(KT[m], KT[m][:], KS[m], KS[m][:], ENB[m], ENB[m][:], ALU.mult)
                        sub(3)
                        for c4 in range(4):
                            bk = bank()
                            bb = bf(bk)
                            for m in range(2):
                                tr(bk, bb[:, m * 128:(m + 1) * 128], KT[m], KT[m][:, c4 * 128:(c4 + 1) * 128], CBt, IDB)
                            for h in range(4):
                                tr(bk, bb[:, 256 + h * 128:256 + (h + 1) * 128], VBF[h], VBF[h][:, c4 * 128:(c4 + 1) * 128], CBt, IDB, sig=(h == 3))
                            cp("act" if c4 % 2 else "dve", KVT[c4], KVT[c4][:], bk, bb[:, 0:768])
                        sub(4)
                        for c4 in range(4):
                            bsc = [bank(), bank()]
                            for hp in range(2):
                                hb = hp * 64
                                for m in range(2):
                                    for par in range(2):
                                        c = 2 * c4 + par
                                        cb = par * 64
                                        mm(bsc[hp], bsc[hp][cb:cb + 64, m * 64:(m + 1) * 64], KT[m], KT[m][hb:hb + 64, c * 64:(c + 1) * 64],
                                           QT[m], QT[m][hb:hb + 64, c * 64:(c + 1) * 64], sig=(m == 1 and par == 1))
                            for hp in range(2):
                                at_v = AT[c4][:, :].rearrange("p (m q i) -> p m q i", q=2, i=64)[:, :, hp, :]
                                tt(AT[c4], at_v, bsc[hp], bsc[hp][:, 0:128].rearrange("p (m i) -> p m i", i=64),
                                   CBt, CBt[:, CB_MG:CB_MG + 128].rearrange("p (m i) -> p m i", i=64), ALU.mult)
                        sub(5)
                        for cc in range(4):
                            cs = (cc, 4 + cc)
                            bos, bkvs = {}, {}
                            for si, c in enumerate(cs):
                                c4, par = divmod(c, 2)
                                cb = par * 64
                                bo, boa = bank(), bank()
                                same = [h for h in range(4) if (h % 2) * 64 == cb]
                                diff = [h for h in range(4) if (h % 2) * 64 != cb]
                                for h in same + diff:
                                    m, hb = h // 2, (h % 2) * 64
                                    mm(bo, bo[:, h * 64:(h + 1) * 64], KVT[c4], KVT[c4][cb:cb + 64, 256 + h * 128:256 + (h + 1) * 128],
                                       AT[c4], AT[c4][cb:cb + 64, h * 64:(h + 1) * 64], start=True, stop=(h in diff), sig=(h == diff[-1]))
                                    if h in same:
                                        mm(bo, bo[:, h * 64:(h + 1) * 64], GSTB[si], GSTB[si][hb:hb + 64, m, :],
                                           QT[m], QT[m][hb:hb + 64, c * 64:(c + 1) * 64], start=False, stop=True)
                                for h in diff:
                                    m, hb = h // 2, (h % 2) * 64
                                    mm(boa, boa[:, h * 64:(h + 1) * 64], GSTB[si], GSTB[si][hb:hb + 64, m, :],
                                       QT[m], QT[m][hb:hb + 64, c * 64:(c + 1) * 64], start=True, stop=True, sig=(h == diff[-1]))
                                bos[si] = (bo, boa)
                                bkv = bank()
                                for h in range(4):
                                    m, hb = h // 2, (h % 2) * 64
                                    mm(bkv, bkv[hb:hb + 64, m * 128:(m + 1) * 128], KVT[c4], KVT[c4][cb:cb + 64, h * 64:(h + 1) * 64],
                                       KVT[c4], KVT[c4][cb:cb + 64, 256 + h * 128:256 + (h + 1) * 128], sig=(h == 3))
                                bkvs[si] = bkv
                            for si, c in enumerate(cs):
                                (bo, boa), bkv = bos[si], bkvs[si]
                                hpd = 1 - (c % 2)
                                cp("act", OTA, OTA[:, :, c * 64:(c + 1) * 64], bo, bo[:, 0:256].rearrange("p (h i) -> p h i", i=64))
                                o_v = OTA[:, :, c * 64:(c + 1) * 64].rearrange("p (m q) i -> p m q i", q=2)[:, :, hpd, :]
                                a_v = boa[:, 0:256].rearrange("p (m q i) -> p m q i", q=2, i=64)[:, :, hpd, :]
                                tt(OTA, o_v, boa, a_v, OTA, o_v, ALU.add)
                                tt(TMP[si], TMP[si][:], GST[si], GST[si][:], bkv, bkv[:, 0:256].rearrange("p (m e) -> p m e", e=128), ALU.add)
                                for m in range(2):
                                    ts(GST[si], GST[si][:, m, :], TMP[si], TMP[si][:, m, :], EB[m][:, c * 64 + 63:c * 64 + 64], ALU.mult, rd=[EB[m]])
                                cp("act", GSTB[si], GSTB[si][:], GST[si], GST[si][:])
                        sub(6)
                        def gdrain(*gens):
                            gens = list(gens)
                            while gens:
                                for g in list(gens):
                                    try:
                                        next(g)
                                    except StopIteration:
                                        gens.remove(g)

                        nSQ = [QT[0], QT[1], KT[0], KT[1]]
                        nRS = [QS[0], QS[1], KS[0], KS[1]]
                        nT1 = [EB[0], EB[1], ENB[0], ENB[1]]

                        def gnorm(h):
                            SQh, RSh, T1h = nSQ[h], nRS[h], nT1[h]
                            act(AF.Square, SQh, SQh[:], OTA, OTA[:, h, :])
                            yield
                            bk = bank()
                            mm(bk, bk[:], CBt, ONES, SQh, SQh[:], sig=True)
                            yield
                            act(AF.Ln, RSh, RSh[:], bk, bk[:], scale=1.0 / 128.0, bias=EPS[:, 0:1], rd=[EPS])
                            yield
                            act(AF.Exp, RSh, RSh[:], RSh, RSh[:], scale=-0.5)
                            yield
                            stt(T1h, T1h[:], OTA, OTA[:, h, :], col(C_NG), RSh, RSh[:], ALU.mult, ALU.mult, rd=[CC])
                            yield
                            tt(OTB, OTB[:, h, :], T1h, T1h[:], SG[h], SG[h][:], ALU.mult)

                        gdrain(gnorm(0), gnorm(1), gnorm(2), gnorm(3))

                    try:
                        chk(2)
                        with Phase(S, nc) as ph:
                            GG = [ph.sb("gg%d" % j, [128, NB], F32) for j in range(4)]
                            BON = [ph.sb("bon%d" % j, [128, NB], F32) for j in range(4)]
                            AR = [ph.sb("ar%d" % j, [128, 2, NB], BF16) for j in range(4)]
                            BK = [ph.sb("bk%d" % j, [128, 2, NB], BF16) for j in range(4)]
                            VB = [ph.sb("vb%d" % j, [128, NB], BF16) for j in range(4)]
                            ELC = ph.sb("elc", [128, 4, 8], F32)
                            p2 = Phase(S, nc)
                            p2.__enter__()
                            ZR = [p2.sb("zr%d" % j, [128, 2, 4 + NB // 2], F32) for j in range(2)]
                            ZS = [p2.sb("zs%d" % j, [128, NB], F32) for j in range(14)]
                            LRB = p2.sb("lrb", [64, NB], BF16)
                            GLB = p2.sb("glb", [96, NB], BF16)
                            D = p2.sb("dd", [128, NB], F32)


                            def zchunk(wb, c0, npart, zi):
                                bk = bank()
                                for k in range(8):
                                    mm(bk, bk[0:npart, :], wb[k], wb[k][:, c0:c0 + npart], XNK[k], XNK[k][:, :], start=(k == 0), stop=(k == 7), sig=(k == 7))
                                z = ZR[zi % 2]
                                hn = NB // 2
                                cp("dve", z, z[0:npart, :, 0:4], RC, RC[0:npart, zi, :, :])
                                cp("act", z, z[0:npart, :, 4:4 + hn], bk, bk[0:npart, :].rearrange("p (s t) -> p s t", s=2))
                                cp("dve", RC, RC[0:npart, zi, :, :], z, z[0:npart, :, hn:hn + 4])
                                mucol = C_MU + zi if zi < 12 else (C_MULR if zi == 12 else C_MUG)
                                D3 = D[0:npart, :].rearrange("p (s t) -> p s t", s=2)
                                ZS3 = ZS[zi][0:npart, :].rearrange("p (s t) -> p s t", s=2)
                                tt(D, D3, z, z[0:npart, :, 3:3 + hn], z, z[0:npart, :, 4:4 + hn], ALU.subtract)
                                stt(ZS[zi], ZS3, D, D3, CC[0:npart, mucol:mucol + 1], z, z[0:npart, :, 4:4 + hn], ALU.mult, ALU.add, rd=[CC])

                            for j in range(12):
                                if j % 4 == 0:
                                    wb = next_unit()
                                    prefetch()
                                zchunk(wb, (j % 4) * 128, 128, j)
                            wb = next_unit()
                            prefetch()
                            zchunk(wb, 0, 64, 12)
                            zchunk(wb, 64, 96, 13)
                            act(AF.Tanh, LRB, LRB[0:32, :], ZS[12], ZS[12][0:32, :])
                            cp("act", LRB, LRB[32:64, :], ZS[12], ZS[12][32:64, :])
                            act(AF.Sigmoid, GLB, GLB[:], ZS[13], ZS[13][0:96, :])

                            if True:
                                scr = [{nm: p2.sb("%s%d" % (nm, q), [128, NB], (BF16 if nm == "sqr" else F32))
                                        for nm in ("sgm", "aa", "lp", "el", "enl", "elm", "kkr", "sqr", "rn", "kk")} for q in range(2)]
                                def bind(m):
                                    sc_ = scr[m % 2]
                                    return (sc_["sgm"], sc_["aa"], sc_["lp"], sc_["el"], sc_["enl"], sc_["elm"], sc_["kkr"], sc_["sqr"], sc_["rn"], sc_["kk"])

                                def stage_a(m):
                                    SGm, AA, LP, EL, ENL, ELM, KKR, SQ, RN, KK = bind(m)
                                    zr, zk, zv = ZS[m], ZS[4 + m], ZS[8 + m]
                                    bk = bank()
                                    mm(bk, bk[:], W2T, W2T[0:32, m * 128:(m + 1) * 128], LRB, LRB[0:32, :], sig=True)
                                    yield
                                    act(AF.Sigmoid, SGm, SGm[:], bk, bk[:], bias=col(C_W0 + m), rd=[CC])
                                    yield
                                    bk = bank()
                                    mm(bk, bk[:], A2T, A2T[32:64, m * 128:(m + 1) * 128], LRB, LRB[32:64, :], sig=True)
                                    yield
                                    act(AF.Sigmoid, AA, AA[:], bk, bk[:], bias=col(C_A0 + m), rd=[CC])
                                    yield
                                    bk = bank()
                                    mm(bk, bk[:], G2T, G2T[0:96, m * 128:(m + 1) * 128], GLB, GLB[0:96, :], sig=True)
                                    yield
                                    cp("act", GG[m], GG[m][:], bk, bk[:])
                                    yield
                                    S.op("dve", lambda: nc.vector.tensor_tensor_scan(out=LP[:], data0=MSCAN, data1=SGm[:], initial=0.0, op0=ALU.mult, op1=ALU.add),
                                         reads=[CF, SGm], writes=[LP])
                                    act(AF.Exp, EL, EL[:], LP, LP[:], scale=-CDEC)
                                    yield
                                    act(AF.Exp, ENL, ENL[:], LP, LP[:], scale=CDEC)
                                    yield
                                    tt(LP, LP[:], LP, LP[:], SGm, SGm[:], ALU.subtract, e=POOL_E)
                                    yield
                                    act(AF.Exp, ELM, ELM[:], LP, LP[:], scale=-CDEC)
                                    yield
                                    cp("dve", ELC, ELC[:, m, :], EL, EL[:, :].rearrange("p (c t) -> p c t", t=64)[:, :, 63])
                                    yield
                                    ts(KKR, KKR[:], zk, zk[:], col(C_KK + m), ALU.mult, rd=[CC])
                                    yield
                                    act(AF.Square, SQ, SQ[:], KKR, KKR[:])
                                    yield
                                    bk = bank()
                                    mm(bk, bk[:], CBt, BD, SQ, SQ[:], sig=True)
                                    yield
                                    act(AF.Ln, RN, RN[:], bk, bk[:], bias=EPS[:, 3:4], rd=[EPS])
                                    yield
                                    act(AF.Exp, RN, RN[:], RN, RN[:], scale=-0.5)
                                    yield

                                def stage_b(m):
                                    SGm, AA, LP, EL, ENL, ELM, KKR, SQ, RN, KK = bind(m)
                                    TA = KKR
                                    KF = RN
                                    TB = TA
                                    RKB = SQ
                                    zr, zk, zv = ZS[m], ZS[4 + m], ZS[8 + m]
                                    tt(KK, KK[:], KKR, KKR[:], RN, RN[:], ALU.mult)
                                    yield
                                    ts(TA, TA[:], AA, AA[:], col(C_KA + m), ALU.mult, col(C_1MKA + m), ALU.add, rd=[CC])
                                    yield
                                    tt(KF, KF[:], zk, zk[:], TA, TA[:], ALU.mult, e=POOL_E)
                                    yield
                                    stt(AR[m], AR[m][:, 0, :], KK, KK[:], -1.0, ELM, ELM[:], ALU.mult, ALU.mult)
                                    yield
                                    tt(AR[m], AR[m][:, 1, :], zr, zr[:], EL, EL[:], ALU.mult, e=POOL_E)
                                    yield
                                    tt(TB, TB[:], KK, KK[:], AA, AA[:], ALU.mult)
                                    yield
                                    tt(BK[m], BK[m][:, 0, :], TB, TB[:], ENL, ENL[:], ALU.mult)
                                    yield
                                    tt(BK[m], BK[m][:, 1, :], KF, KF[:], ENL, ENL[:], ALU.mult, e=POOL_E)
                                    yield
                                    cp("act", VB[m], VB[m][:], zv, zv[:])
                                    yield
                                    stt(RKB, RKB[:], zr, zr[:], col(C_RK + m), KF, KF[:], ALU.mult, ALU.mult, rd=[CC])
                                    yield
                                    bk = bank()
                                    mm(bk, bk[:], CBt, BD, RKB, RKB[:], sig=True)
                                    yield
                                    tt(BON[m], BON[m][:], bk, bk[:], zv, zv[:], ALU.mult)
                                    yield

                                def drain(*gens):
                                    gens = [g for g in gens if g is not None]
                                    while gens:
                                        for g in list(gens):
                                            try:
                                                next(g)
                                            except StopIteration:
                                                gens.remove(g)

                                drain(stage_a(0))
                                for m in range(4):
                                    drain(stage_a(m + 1) if m + 1 < 4 else None, stage_b(m))

                            p2.__exit__(None, None, None)
                            with Phase(S, nc) as p3:
                                YT = p3.sb("yt", [128, 4, NB], F32)
                                BKT = [p3.sb("bkt%d" % j, [128, 1024], BF16) for j in range(4)]
                                VT = [p3.sb("vt%d" % j, [128, NB], BF16) for j in range(4)]
                                CMB = [p3.sb("cmb%d" % j, [128, 8, 128], BF16) for j in range(4)]
                                CMK = [p3.sb("cmk%d" % j, [128, 8, 128], BF16) for j in range(4)]
                                PV = [p3.sb("pv%d" % j, [128, 8, 2, 64], BF16) for j in range(4)]
                                QQ = [p3.sb("qq%d" % j, [128, 8, 64], BF16) for j in range(4)]
                                X1 = [p3.sb("x1_%d" % i_, [128, 8, 64], BF16) for i_ in range(2)]
                                U1 = [p3.sb("u1_%d" % i_, [128, 8, 64], BF16) for i_ in range(2)]
                                TMPR = [p3.sb("tmpr%d" % i_, [128, 4, 64], F32) for i_ in range(2)]
                                I8v = CBt[:, CB_I8:CB_I8 + 512].rearrange("p (h t) -> p h t", t=64)
                                MQv = CBt[:, CB_MQ:CB_MQ + 512].rearrange("p (h t) -> p h t", t=64)
                                MRWv = CBt[:, CB_MRW:CB_MRW + 512].rearrange("p (h t) -> p h t", t=128)

                                def v3(bk, w):
                                    return bk[:, :].rearrange("p (h t) -> p h t", t=w)

                                for c4 in range(4):
                                    bk = bank()
                                    bb = bf(bk)
                                    for m in range(4):
                                        tr(bk, bb[:, m * 128:(m + 1) * 128], BK[m], BK[m][:, 0, c4 * 128:(c4 + 1) * 128], CBt, IDB)
                                    for m in range(4):
                                        tr(bk, bb[:, 512 + m * 128:512 + (m + 1) * 128], BK[m], BK[m][:, 1, c4 * 128:(c4 + 1) * 128], CBt, IDB, sig=(m == 3))
                                    cp("act", BKT[c4], BKT[c4][:], bk, bb[:, :])
                                    bk = bank()
                                    bb = bf(bk)
                                    for m in range(4):
                                        tr(bk, bb[:, m * 128:(m + 1) * 128], VB[m], VB[m][:, c4 * 128:(c4 + 1) * 128], CBt, IDB, sig=(m == 3))
                                    cp("dve", VT[c4], VT[c4][:], bk, bb[:, 0:512])
                                for c4 in range(4):
                                    bB = [bank(), bank()]
                                    bK = [bank(), bank()]
                                    bQ = [bank(), bank()]
                                    for hp in range(2):
                                        hb = hp * 64
                                        for m in range(4):
                                            h = 2 * m + hp
                                            for par in range(2):
                                                c = 2 * c4 + par
                                                cb = par * 64
                                                last = (m == 3 and par == 1)
                                                rhs_ar = AR[m][hb:hb + 64, :, c * 64:(c + 1) * 64]
                                                mm(bB[hp], bB[hp][cb:cb + 64, m * 128:(m + 1) * 128], BK[m], BK[m][hb:hb + 64, 0, c * 64:(c + 1) * 64], AR[m], rhs_ar, sig=last)
                                                mm(bK[hp], bK[hp][cb:cb + 64, m * 128:(m + 1) * 128], BK[m], BK[m][hb:hb + 64, 1, c * 64:(c + 1) * 64], AR[m], rhs_ar, sig=last)
                                                mm(bQ[hp], bQ[hp][cb:cb + 64, m * 64:(m + 1) * 64], AR[m], AR[m][hb:hb + 64, 0, c * 64:(c + 1) * 64],
                                                   BK[m], BK[m][hb:hb + 64, 0, c * 64:(c + 1) * 64], sig=last)
                                    for hp in range(2):
                                        cmb_v = CMB[c4][:, :, :].rearrange("p (m q) t -> p m q t", q=2)[:, :, hp, :]
                                        cmk_v = CMK[c4][:, :, :].rearrange("p (m q) t -> p m q t", q=2)[:, :, hp, :]
                                        qq_v = QQ[c4][:, :, :].rearrange("p (m q) t -> p m q t", q=2)[:, :, hp, :]
                                        tt(CMB[c4], cmb_v, bB[hp], v3(bB[hp], 128), CBt, MRWv, ALU.mult)
                                        tt(CMK[c4], cmk_v, bK[hp], v3(bK[hp], 128), CBt, MRWv, ALU.mult)
                                        tt(QQ[c4], qq_v, bQ[hp], bQ[hp][:, 0:256].rearrange("p (m t) -> p m t", t=64), CBt, MQv[:, 0:4, :], ALU.mult)
                                    cp("act", PV[c4], PV[c4][:, :, 0, :], CMB[c4], CMB[c4][:, :, 0:64])
                                    cp("act", PV[c4], PV[c4][:, :, 1, :], CBt, I8v)
                                for kd in range(6):
                                    for half in range(2):
                                        c4s = (2 * half, 2 * half + 1)
                                        bks = {}
                                        for c4 in c4s:
                                            ba, bb_ = bank(), (bank() if kd < 5 else None)
                                            b2_ = bank() if kd < 5 else None
                                            for par in range(2):
                                                cb = par * 64
                                                for h in range(8):
                                                    last = (par == 1 and h % 4 == 3)
                                                    Qk = QQ[c4][cb:cb + 64, h, :]
                                                    if kd < 5:
                                                        ob = ba if h < 4 else bb_
                                                        hl = h % 4
                                                        mm(ob, ob[cb:cb + 64, hl * 128:(hl + 1) * 128], QQ[c4], Qk, PV[c4], PV[c4][cb:cb + 64, h, :, :], sig=last, force=False)
                                                        mm(b2_, b2_[cb:cb + 64, h * 64:(h + 1) * 64], PV[c4], PV[c4][cb:cb + 64, h, 0, :], QQ[c4], Qk, sig=(par == 1 and h == 7), force=False)
                                                    else:
                                                        mm(ba, ba[cb:cb + 64, h * 64:(h + 1) * 64], QQ[c4], Qk, PV[c4], PV[c4][cb:cb + 64, h, 1, :], sig=(par == 1 and h == 7), force=False)
                                            bks[c4] = (ba, bb_, b2_)
                                        for c4 in c4s:
                                            ba, bb_, b2_ = bks[c4]
                                            if kd < 5:
                                                for hh, bx_ in ((0, ba), (4, bb_)):
                                                    cp("act", PV[c4], PV[c4][:, hh:hh + 4, 0, :], bx_, v3(bx_, 128)[:, :, 0:64])
                                                    tt(PV[c4], PV[c4][:, hh:hh + 4, 1, :], bx_, v3(bx_, 128)[:, :, 64:128], PV[c4], PV[c4][:, hh:hh + 4, 1, :], ALU.add)
                                                cp("dve", QQ[c4], QQ[c4][:], b2_, v3(b2_, 64))
                                            else:
                                                tt(PV[c4], PV[c4][:, :, 1, :], ba, v3(ba, 64), PV[c4], PV[c4][:, :, 1, :], ALU.add)
                                for cc in range(4):
                                    cs = (cc, 4 + cc)
                                    q0, q1 = cc % 2, (cc + 1) % 2
                                    bxs, bus, bts, bys = {}, {}, {}, {}
                                    for si, c in enumerate(cs):
                                        c4, par = divmod(c, 2)
                                        cb = par * 64
                                        bx, bxa = bank(), bank()
                                        same = [h for h in range(8) if (h % 2) * 64 == cb]
                                        diff = [h for h in range(8) if (h % 2) * 64 != cb]
                                        for h in same:
                                            m, hb = h // 2, (h % 2) * 64
                                            mm(bx, bx[cb:cb + 64, h * 64:(h + 1) * 64], AR[m], AR[m][hb:hb + 64, 0, c * 64:(c + 1) * 64],
                                               TSB[si][q0], TSB[si][q0][hb:hb + 64, m, :], start=True, stop=False)
                                            mm(bx, bx[cb:cb + 64, h * 64:(h + 1) * 64], CMK[c4], CMK[c4][cb:cb + 64, h, 0:64],
                                               VT[c4], VT[c4][cb:cb + 64, h * 64:(h + 1) * 64], start=False, stop=True)
                                        for h in diff:
                                            mm(bx, bx[cb:cb + 64, h * 64:(h + 1) * 64], CMK[c4], CMK[c4][cb:cb + 64, h, 0:64],
                                               VT[c4], VT[c4][cb:cb + 64, h * 64:(h + 1) * 64], start=True, stop=True, sig=(h == diff[-1]))
                                        for h in diff:
                                            m, hb = h // 2, (h % 2) * 64
                                            mm(bxa, bxa[cb:cb + 64, h * 64:(h + 1) * 64], AR[m], AR[m][hb:hb + 64, 0, c * 64:(c + 1) * 64],
                                               TSB[si][q0], TSB[si][q0][hb:hb + 64, m, :], start=True, stop=True, sig=(h == diff[-1]))
                                        bxs[si] = (bx, bxa)
                                    for si, c in enumerate(cs):
                                        cb = (c % 2) * 64
                                        bx, bxa = bxs[si]
                                        hpd = 1 - (c % 2)
                                        cp("act", X1[si], X1[si][cb:cb + 64, :, :], bx, bx[cb:cb + 64, :].rearrange("p (h t) -> p h t", t=64))
                                        xv = X1[si][cb:cb + 64, :, :].rearrange("p (m q) t -> p m q t", q=2)[:, :, hpd, :]
                                        av = bxa[cb:cb + 64, :].rearrange("p (m q t) -> p m q t", q=2, t=64)[:, :, hpd, :]
                                        tt(X1[si], xv, bxa, av, X1[si], xv, ALU.add)
                                    for si, c in enumerate(cs):
                                        c4, par = divmod(c, 2)
                                        cb = par * 64
                                        bu = bank()
                                        for h in range(8):
                                            mm(bu, bu[cb:cb + 64, h * 64:(h + 1) * 64], PV[c4], PV[c4][cb:cb + 64, h, 1, :], X1[si], X1[si][cb:cb + 64, h, :], sig=(h == 7), force=False)
                                        bus[si] = bu
                                    for si, c in enumerate(cs):
                                        cb = (c % 2) * 64
                                        bu = bus[si]
                                        cp("dve", U1[si], U1[si][cb:cb + 64, :, :], bu, bu[cb:cb + 64, :].rearrange("p (h t) -> p h t", t=64))
                                    for si, c in enumerate(cs):
                                        c4, par = divmod(c, 2)
                                        cb = par * 64
                                        bt = bank()
                                        for h in range(8):
                                            m, hb = h // 2, (h % 2) * 64
                                            o_ap = bt[hb:hb + 64, m * 64:(m + 1) * 64]
                                            mm(bt, o_ap, BKT[c4], BKT[c4][cb:cb + 64, h * 64:(h + 1) * 64], U1[si], U1[si][cb:cb + 64, h, :], start=True, stop=False)
                                            mm(bt, o_ap, BKT[c4], BKT[c4][cb:cb + 64, 512 + h * 64:512 + (h + 1) * 64], VT[c4], VT[c4][cb:cb + 64, h * 64:(h + 1) * 64],
                                               start=False, stop=True, sig=(h == 7))
                                        bts[si] = bt
                                    for si, c in enumerate(cs):
                                        bt = bts[si]
                                        tt(TMPR[si], TMPR[si][:], TS[si], TS[si][:], bt, bt[:, 0:256].rearrange("p (m i) -> p m i", i=64), ALU.add)
                                        tt(TS[si], TS[si][:], TMPR[si], TMPR[si][:], ELC, ELC[:, :, c:c + 1].broadcast_to([128, 4, 64]), ALU.mult)
                                        cp("act", TSB[si][q1], TSB[si][q1][:], TS[si], TS[si][:])
                                    for si, c in enumerate(cs):
                                        c4, par = divmod(c, 2)
                                        cb = par * 64
                                        by, bya = bank(), bank()
                                        same = [h for h in range(8) if (h % 2) * 64 == cb]
                                        diff = [h for h in range(8) if (h % 2) * 64 != cb]
                                        for h in same + diff:
                                            m, hb = h // 2, (h % 2) * 64
                                            o_ap = by[hb:hb + 64, m * 64:(m + 1) * 64]
                                            if h in same:
                                                mm(by, o_ap, TSB[si][q0], TSB[si][q0][hb:hb + 64, m, :], AR[m], AR[m][hb:hb + 64, 1, c * 64:(c + 1) * 64], start=True, stop=False)
                                            mm(by, o_ap, U1[si], U1[si][cb:cb + 64, h, :], CMB[c4], CMB[c4][cb:cb + 64, h, 64:128], start=(h in diff), stop=False)
                                            mm(by, o_ap, VT[c4], VT[c4][cb:cb + 64, h * 64:(h + 1) * 64], CMK[c4], CMK[c4][cb:cb + 64, h, 64:128], start=False, stop=True, sig=(h == diff[-1]))
                                        for h in diff:
                                            m, hb = h // 2, (h % 2) * 64
                                            mm(bya, bya[hb:hb + 64, m * 64:(m + 1) * 64], TSB[si][q0], TSB[si][q0][hb:hb + 64, m, :], AR[m], AR[m][hb:hb + 64, 1, c * 64:(c + 1) * 64],
                                               start=True, stop=True, sig=(h == diff[-1]))
                                        bys[si] = (by, bya)
                                    for si, c in enumerate(cs):
                                        by, bya = bys[si]
                                        hd = (1 - (c % 2)) * 64
                                        cp("act", YT, YT[:, :, c * 64:(c + 1) * 64], by, by[:, 0:256].rearrange("p (m t) -> p m t", t=64))
                                        tt(YT, YT[hd:hd + 64, :, c * 64:(c + 1) * 64], bya, bya[hd:hd + 64, 0:256].rearrange("p (m t) -> p m t", t=64),
                                           YT, YT[hd:hd + 64, :, c * 64:(c + 1) * 64], ALU.add)

                                with Phase(S, nc) as p4:
                                    def alias(tl, dt):
                                        nt = T(tl.t[:].bitcast(dt), tl.name + "_al")
                                        nt.w, nt.r = tl.w, dict(tl.r)
                                        return nt
                                    def alias2(tl, ap, nm):
                                        nt = T(ap, nm)
                                        nt.w, nt.r = tl.w, dict(tl.r)
                                        return nt

                                    def gn(m):
                                        YB = VT[m]
                                        SQ = alias2(CMK[m], CMK[m].t[:, 0:4, :].rearrange("p a b -> p (a b)"), "sq4")
                                        YC = alias(BKT[m], F32)
                                        RS = alias2(CMB[m], CMB[m].t[:, :, :].rearrange("p a b -> p (a b)").bitcast(F32), "rs4")
                                        cp("act", YB, YB[:], YT, YT[:, m, :])
                                        yield
                                        bk = bank()
                                        mm(bk, bk[:], CBt, BD, YB, YB[:], sig=True)
                                        yield
                                        stt(YC, YC[:], bk, bk[:], -1.0 / 64.0, YT, YT[:, m, :], ALU.mult, ALU.add)
                                        yield
                                        act(AF.Square, SQ, SQ[:], YC, YC[:])
                                        yield
                                        bk = bank()
                                        mm(bk, bk[:], CBt, BD, SQ, SQ[:], sig=True)
                                        yield
                                        act(AF.Ln, RS, RS[:], bk, bk[:], scale=1.0 / 64.0, bias=EPS[:, 1:2], rd=[EPS])
                                        yield
                                        act(AF.Exp, RS, RS[:], RS, RS[:], scale=-0.5)
                                        yield
                                        tt(YC, YC[:], YC, YC[:], RS, RS[:], ALU.mult)
                                        yield
                                        ts(YC, YC[:], YC, YC[:], col(C_GNW + m), ALU.mult, col(C_GNB + m), ALU.add, rd=[CC])
                                        yield
                                        tt(YC, YC[:], YC, YC[:], BON[m], BON[m][:], ALU.add)
                                        yield
                                        tt(OTB, OTB[:, 4 + m, :], YC, YC[:], GG[m], GG[m][:], ALU.mult)

                                    drain(gn(0), gn(1), gn(2), gn(3))

                    except _Skip:
                        memset(OTB, OTB[:, 4:8, :], 0.0)
                        for _u in range(4):
                            next_unit()
                            prefetch()
                    for j in range(8):
                        if j % 4 == 0:
                            wb = next_unit()
                            prefetch()
                        bk = bank()
                        for k in range(8):
                            mm(bk, bk[:], wb[k], wb[k][:, (j % 4) * 128:(j % 4 + 1) * 128], OTB, OTB[:, k, :], start=(k == 0), stop=(k == 7), sig=(k == 7))
                        tt(H[j], H[j][:], bk, bk[:], H[j], H[j][:], ALU.add)

                chk(2)
                chk(3)
                with Sub(get_tail()) as ph:
                    XNK = ksplit(ph.sb("XN2", [128, 8, NB], BF16))
                    HID = [ph.sb("hid%d" % i, [128, 8, NB], BF16) for i in range(2)]
                    RL = [ph.sb("rl%d" % i, [128, NB], F32) for i in range(2)]
                    with Sub(get_tail()) as p1:
                        rmsnorm(p1, C_LNMLP, XNK, None)
                    if b + 1 < nblk and INTERLEAVE_PROLOGUE:
                        pro[0] = prologue(b + 1)

                    def step():
                        if pro[0] is not None:
                            try:
                                next(pro[0])
                            except StopIteration:
                                pro[0] = None

                    for g in range(4):
                        hid = HID[g % 2]
                        bks8 = dense8_kouter(XNK) if g == 0 else None
                        for j in range(8):
                            if bks8 is not None:
                                bk = bks8[j]
                            else:
                                if j % 4 == 0:
                                    wb = next_unit()
                                    prefetch()
                                bk = bank()
                                for k in range(8):
                                    mm(bk, bk[:], wb[k], wb[k][:, (j % 4) * 128:(j % 4 + 1) * 128], XNK[k], XNK[k][:, :], start=(k == 0), stop=(k == 7), sig=(k == 7))
                            rl = RL[j % 2]
                            act(AF.Relu, rl, rl[:], bk, bk[:])
                            tt(hid, hid[:, j, :], rl, rl[:], rl, rl[:], ALU.mult)
                            step()
                        for j in range(8):
                            if j % 4 == 0:
                                wb = next_unit()
                                prefetch()
                            bk = bank()
                            for k in range(8):
                                mm(bk, bk[:], wb[k], wb[k][:, (j % 4) * 128:(j % 4 + 1) * 128], hid, hid[:, k, :], start=(k == 0), stop=(k == 7), sig=(k == 7))
                            tt(H[j], H[j][:], bk, bk[:], H[j], H[j][:], ALU.add)
                            step()

                chk(4)
                with Sub(get_tail()) as ph:
                    XNK = ksplit(ph.sb("XN3", [128, 8, NB], BF16))
                    G = [ph.sb("pg%d" % j, [128, NB], F32) for j in range(8)]
                    TM = [ph.sb("ptm%d" % j, [128, NB], F32) for j in range(2)]
                    with Sub(get_tail()) as p1:
                        rmsnorm(p1, C_LNPLE, XNK, None)
                    bks8 = dense8_kouter(XNK)
                    for j in range(8):
                        bk = bks8[j]
                        act(AF.Sigmoid, G[j], G[j][:], bk, bk[:])
                    for j in range(8):
                        if j % 4 == 0:
                            wb = next_unit()
                            prefetch()
                        bk = bank()
                        for k in range(2):
                            mm(bk, bk[:], wb[k], wb[k][:, (j % 4) * 128:(j % 4 + 1) * 128], PT, PT[:, k, :], start=(k == 0), stop=(k == 1), sig=(k == 1))
                        tm = TM[j % 2]
                        tt(tm, tm[:], bk, bk[:], G[j], G[j][:], ALU.mult)
                        tt(H[j], H[j][:], tm, tm[:], H[j], H[j][:], ALU.add, e="dve")

            except _Skip:
                pass
            with Sub(get_tail()) as ph:
                XF = ph.sb("XF", [128, 8, NB], F32)
                with Sub(get_tail()) as p1:
                    rmsnorm(p1, C_LNF, XF, lambda k: XF[:, k, :])
                for tq in range(4):
                    xs = XS[tq % 2]
                    b0, b1 = bank(), bank()
                    for k in range(8):
                        ob = b0 if k < 4 else b1
                        tr(ob, ob[:, (k % 4) * 128:(k % 4 + 1) * 128], XF, XF[:, k, tq * 128:(tq + 1) * 128], CF, IDF, sig=(k % 4 == 3))
                    cp("act", xs, xs[:, 0:512], b0, b0[:])
                    cp("dve", xs, xs[:, 512:1024], b1, b1[:])
                    r0 = (tq // 2) * (TOK // 2) + b * (NB // 2) + (tq % 2) * 128
                    out_dmas.append(S.dma("sp", out[r0:r0 + 128, :], xs[:], reads=[xs]))
            tailp[0].__exit__(None, None, None)
            tailp[0] = None

        S.wait_all_dma("sp")
        S.barrier(("pe", "act", "dve", "pool"))
    return nc


_NC_CACHE = {}


def kernel(**inputs):
    i = {k: np.asarray(v) for k, v in inputs.items()}
    x = np.ascontiguousarray(i["x"], dtype=np.float32)
    p = np.ascontiguousarray(i["p"], dtype=np.float32)[0]
    cb, cf = _const_arrays()
    shared = {
        "w_in": np.ascontiguousarray(i["w_in"][0], dtype=np.float32),
        "w_out": np.ascontiguousarray(i["w_out"][0], dtype=np.float32),
        "w_ff1": np.ascontiguousarray(i["w_ff1"][0], dtype=np.float32),
        "w_ff2": np.ascontiguousarray(i["w_ff2"][0], dtype=np.float32),
        "w_pg": np.ascontiguousarray(i["w_ple_gate"][0], dtype=np.float32),
        "w_pp": np.ascontiguousarray(i["w_ple_proj"][0], dtype=np.float32),
        "gk_w": np.ascontiguousarray(i["gla_gk_w"][0], dtype=np.float32),
        "w2": np.ascontiguousarray(i["rwkv_w2"][0], dtype=np.float32),
        "a2": np.ascontiguousarray(i["rwkv_a2"][0], dtype=np.float32),
        "g2": np.ascontiguousarray(i["rwkv_g2"][0], dtype=np.float32),
        "vecs": _pack_vecs(i),
        "cb": cb,
        "cf": cf,
    }
    in_maps = []
    for c in range(NCORES):
        m = dict(shared)
        m["x"] = x[2 * c:2 * c + 2].reshape(TOK, 1024)
        m["p"] = p[2 * c:2 * c + 2].reshape(TOK, 256)
        in_maps.append(m)
    if "nc" not in _NC_CACHE:
        _NC_CACHE["nc"] = build_nc()
    res = run_bass_kernel_spmd(_NC_CACHE["nc"], in_maps, core_ids=list(range(NCORES)))
    outs = [np.asarray(r["out"], dtype=np.float32).reshape(2, 2048, 1024) for r in res.results]
    return np.concatenate(outs, axis=0)
```
